# Optimizing a Trainium2 kernel written in Bass

```python
import math
import jax, jax.numpy as jnp
from jax import lax
import numpy as np

D_MODEL = 1024
BATCH = 1
SEQ = 16384
DEPTH = 1

HEAD_DIM = 64
ATT_GROUPS = ((128, 1), (512, 4), (2048, 16))
HEADS_PER_GROUP = 4
N_ATT_HEADS = HEADS_PER_GROUP * len(ATT_GROUPS)
ATT_WIDTH = N_ATT_HEADS * HEAD_DIM
ATT_MERGED = HEADS_PER_GROUP * HEAD_DIM
BLK = 128
POOL_WINDOWS = (2, 4, 8, 16)
POOL_GROUP_WIDTH = 3 * D_MODEL // 16
POOL_WIDTH = POOL_GROUP_WIDTH * len(POOL_WINDOWS)
D_FF = 4 * D_MODEL
N_IN = 3 * ATT_WIDTH + POOL_WIDTH + 2 * D_MODEL
NORM_EPS = 1e-6
ALIBI_MAX_BIAS = 8.0

kernel_name = "hybrid_dilated_attn_pool_gated_block"


def _rmsnorm(x, g):
    xf = x.astype(jnp.float32)
    y = xf * lax.rsqrt(jnp.mean(xf * xf, axis=-1, keepdims=True) + NORM_EPS)
    return (y * g.astype(jnp.float32)).astype(x.dtype)


def _dilated_window_attention(q, k, v, dilation, n_steps, slopes):
    B, S, H, Dh = q.shape
    L = S // dilation
    nb = -(-L // BLK)
    Lp = nb * BLK
    Z = B * dilation

    def to_sub(t):
        t = t.reshape(B, L, dilation, H, Dh).transpose(0, 2, 1, 3, 4).reshape(Z, L, H, Dh)
        return jnp.pad(t, ((0, 0), (0, Lp - L), (0, 0), (0, 0)))

    def band(t):
        prev = jnp.pad(t, ((0, 0), (BLK, 0), (0, 0), (0, 0)))[:, :Lp]
        return jnp.concatenate([prev.reshape(Z, nb, BLK, H, Dh),
                                t.reshape(Z, nb, BLK, H, Dh)], axis=2)

    qb = to_sub(q).reshape(Z, nb, BLK, H, Dh).astype(jnp.float32)
    kb = band(to_sub(k)).astype(jnp.float32)
    vb = band(to_sub(v)).astype(jnp.float32)

    s = jnp.einsum('znqhd,znkhd->znhqk', qb, kb) * (Dh ** -0.5)
    steps = BLK + jnp.arange(BLK)[:, None] - jnp.arange(2 * BLK)[None, :]
    key_idx = (jnp.arange(nb)[:, None, None] * BLK
               + jnp.arange(2 * BLK)[None, None, :] - BLK)
    valid = (steps >= 0) & (steps <= n_steps) & (key_idx >= 0)
    bias = -(slopes[:, None, None] * (steps * dilation).astype(jnp.float32))
    s = jnp.where(valid[None, :, None], s + bias[None, None], -jnp.inf)
    lse = jax.nn.logsumexp(s, axis=-1)
    p = jnp.exp(s - lse[..., None])
    o = jnp.einsum('znhqk,znkhd->znqhd', p, vb).reshape(Z, Lp, H, Dh)[:, :L]
    lse = lse.transpose(0, 1, 3, 2).reshape(Z, Lp, H)[:, :L]
    o = o.reshape(B, dilation, L, H, Dh).transpose(0, 2, 1, 3, 4).reshape(B, S, H, Dh)
    lse = lse.reshape(B, dilation, L, H).transpose(0, 2, 1, 3).reshape(B, S, H)
    return o, lse


def _attention_branch(q, k, v):
    B, S, _ = q.shape
    q = q.reshape(B, S, N_ATT_HEADS, HEAD_DIM)
    k = k.reshape(B, S, N_ATT_HEADS, HEAD_DIM)
    v = v.reshape(B, S, N_ATT_HEADS, HEAD_DIM)
    slopes = 2.0 ** (-ALIBI_MAX_BIAS * (jnp.arange(N_ATT_HEADS, dtype=jnp.float32) + 1.0)
                     / N_ATT_HEADS)
    outs, lses = [], []
    for g, (window, dilation) in enumerate(ATT_GROUPS):
        hs = slice(g * HEADS_PER_GROUP, (g + 1) * HEADS_PER_GROUP)
        o, l = _dilated_window_attention(q[:, :, hs], k[:, :, hs], v[:, :, hs],
                                         dilation, window // dilation, slopes[hs])
        outs.append(o)
        lses.append(l)
    outs = jnp.stack(outs, axis=0)
    wts = jax.nn.softmax(jnp.stack(lses, axis=0), axis=0)
    merged = jnp.sum(wts[..., None] * outs, axis=0)
    return merged.reshape(B, S, ATT_MERGED)


def _pool_branch(pz, w_grp, scale):
    B, S, _ = pz.shape
    pf = pz.astype(jnp.float32).reshape(B, S, len(POOL_WINDOWS), POOL_GROUP_WIDTH)
    c0 = jnp.pad(jnp.cumsum(pf, axis=1), ((0, 0), (1, 0), (0, 0), (0, 0)))
    t = jnp.arange(S)
    pooled = []
    for g, w in enumerate(POOL_WINDOWS):
        lower = jnp.take(c0[:, :, g], jnp.maximum(t + 1 - w, 0), axis=1)
        count = jnp.minimum(t + 1, w).astype(jnp.float32)[None, :, None]
        pooled.append((c0[:, 1:, g] - lower) / count)
    pooled = jnp.stack(pooled, axis=2) - pf
    mixed = jnp.einsum('bsgc,gcd->bsgd', pooled, w_grp.astype(jnp.float32))
    return mixed.reshape(B, S, POOL_WIDTH) * scale.astype(jnp.float32)


def setup_inputs(seed: int = 0) -> dict:
    key = jax.random.key(seed)
    ks = jax.random.split(key, 12)
    f32 = jnp.float32

    def nrm(k, shape, fan_in):
        return jax.random.normal(k, shape, f32) * (fan_in ** -0.5)

    def gain(k, shape):
        return 1.0 + 0.02 * jax.random.normal(k, shape, f32)

    return {
        "x": jax.random.normal(ks[0], (BATCH, SEQ, D_MODEL), f32),
        "norm_mix_g": gain(ks[1], (DEPTH, D_MODEL)),
        "w_in": nrm(ks[2], (DEPTH, D_MODEL, N_IN), D_MODEL),
        "w_att_out": nrm(ks[3], (DEPTH, ATT_MERGED, D_MODEL), ATT_MERGED),
        "w_pool_grp": nrm(ks[4], (DEPTH, len(POOL_WINDOWS), POOL_GROUP_WIDTH, POOL_GROUP_WIDTH),
                          POOL_GROUP_WIDTH),
        "pool_scale": 1.0 + 0.1 * jax.random.normal(ks[5], (DEPTH, POOL_WIDTH), f32),
        "w_pool_out": nrm(ks[6], (DEPTH, POOL_WIDTH, D_MODEL), POOL_WIDTH),
        "w_out": nrm(ks[7], (DEPTH, D_MODEL, D_MODEL), D_MODEL),
        "norm_mlp_g": gain(ks[8], (DEPTH, D_MODEL)),
        "w_mlp_in": nrm(ks[9], (DEPTH, D_MODEL, D_FF), D_MODEL),
        "w_mlp_out": nrm(ks[10], (DEPTH, D_FF, D_MODEL), D_FF),
        "norm_final_g": gain(ks[11], (D_MODEL,)),
    }


def reference(x, norm_mix_g, w_in, w_att_out, w_pool_grp, pool_scale, w_pool_out, w_out,
              norm_mlp_g, w_mlp_in, w_mlp_out, norm_final_g):
    dt = x.dtype
    offs = np.cumsum([ATT_WIDTH, ATT_WIDTH, ATT_WIDTH, POOL_WIDTH, D_MODEL]).tolist()
    h = x
    for l in range(DEPTH):
        u = _rmsnorm(h, norm_mix_g[l])
        z = jnp.einsum('bsd,dn->bsn', u, w_in[l])
        q, k, v, pz, ga, gp = jnp.split(z, offs, axis=-1)
        a = _attention_branch(q, k, v).astype(dt)
        p = _pool_branch(pz, w_pool_grp[l], pool_scale[l]).astype(dt)
        merged = (jax.nn.sigmoid(ga) * jnp.einsum('bsc,cd->bsd', a, w_att_out[l])
                  + jax.nn.sigmoid(gp) * jnp.einsum('bsc,cd->bsd', p, w_pool_out[l]))
        h = h + jnp.einsum('bsd,de->bse', merged, w_out[l])
        m = _rmsnorm(h, norm_mlp_g[l])
        hid = jnp.square(jax.nn.relu(jnp.einsum('bsd,df->bsf', m, w_mlp_in[l])))
        h = h + jnp.einsum('bsf,fd->bsd', hid, w_mlp_out[l])
    return _rmsnorm(h, norm_final_g)
```

```python
from contextlib import ExitStack
import numpy as np
import concourse.bass as bass
import concourse.mybir as mybir
from concourse.bass_utils import run_bass_kernel_spmd

F32 = mybir.dt.float32
BF16 = mybir.dt.bfloat16
ALU = mybir.AluOpType
AF = mybir.ActivationFunctionType

NCORES = 8
TOK = 2048
KB = 1024
ARENA_KB = 207
NEG = -30000.0


def I(name, *a, **k):
    return lambda e: getattr(e, name)(*a, **k)


class Op:
    __slots__ = ("eng", "fn", "deps", "dma_key", "seq", "has_dep", "idx", "is_out")


class Prog:
    ENGS = ("pe", "act", "dve", "pool", "sp")

    def __init__(self):
        self.ops = []
        self.last_w = {}
        self.readers = {}
        self.dma_keys = []
        self.last_eng = {}
        self.last_dma = {}
        self.bar_deps = set()

    def add(self, eng, fn, reads=(), writes=(), dma_key=None, is_out=False):
        op = Op()
        op.eng = eng
        op.fn = fn
        op.idx = len(self.ops)
        op.dma_key = dma_key
        op.has_dep = False
        op.seq = 0
        op.is_out = is_out
        deps = set(self.bar_deps)
        for k in reads:
            w = self.last_w.get(k)
            if w is not None:
                deps.add(w)
        for k in writes:
            w = self.last_w.get(k)
            if w is not None:
                deps.add(w)
            for r in self.readers.get(k, ()):
                deps.add(r)
        op.deps = deps
        for k in reads:
            self.readers.setdefault(k, []).append(op.idx)
        for k in writes:
            self.last_w[k] = op.idx
            self.readers[k] = []
        if dma_key is not None:
            if dma_key not in self.dma_keys:
                self.dma_keys.append(dma_key)
            self.last_dma[dma_key] = op.idx
        else:
            self.last_eng[eng] = op.idx
        self.ops.append(op)
        return op

    def barrier(self):
        self.bar_deps = set(self.last_eng.values()) | set(self.last_dma.values())

    def emit(self, nc, block, sems):
        ops = self.ops
        for op in ops:
            for d in op.deps:
                dop = ops[d]
                if dop.dma_key is None and not (dop.eng == "pe" and op.eng == "pe"):
                    dop.has_dep = True
        cnt = {e: 0 for e in self.ENGS}
        for op in ops:
            if op.dma_key is None and op.has_dep:
                cnt[op.eng] += 1
                op.seq = cnt[op.eng]
        dma_cum = {}
        dma_at = {}
        run = {}
        for op in ops:
            if op.dma_key is not None:
                run[op.dma_key] = run.get(op.dma_key, 0) + 16
                dma_cum.setdefault(op.dma_key, []).append((op.idx, run[op.dma_key]))
                dma_at[op.idx] = run[op.dma_key]
        eng_sem = {e: sems["eng_" + e] for e in self.ENGS}
        dma_sem = {k: sems["dma_%d" % i] for i, k in enumerate(self.dma_keys)}

        def cum_before(key, idx):
            v = 0
            for (i, c) in dma_cum[key]:
                if i < idx:
                    v = c
                else:
                    break
            return v

        def run_engine(eng, e):
            waited = {}
            for op in ops:
                if op.eng != eng:
                    continue
                for d in sorted(op.deps):
                    dop = ops[d]
                    if dop.dma_key is not None:
                        s = dma_sem[dop.dma_key]
                        val = dma_at[dop.idx]
                        key = ("d", dop.dma_key)
                    else:
                        if dop.eng == "pe" and eng == "pe":
                            continue
                        s = eng_sem[dop.eng]
                        val = dop.seq
                        key = ("e", dop.eng)
                    if waited.get(key, 0) >= val:
                        continue
                    e.wait_ge(s, val)
                    waited[key] = val
                inst = op.fn(e)
                if op.dma_key is not None:
                    inst.then_inc(dma_sem[op.dma_key], 16)
                elif op.has_dep:
                    inst.then_inc(eng_sem[eng], 1)
            if eng == "sp":
                for k in self.dma_keys:
                    if any(o.is_out for o in ops if o.dma_key == k):
                        e.wait_ge(dma_sem[k], dma_cum[k][-1][1])

        @block.tensor
        def _(e):
            run_engine("pe", e)

        @block.scalar
        def _(e):
            run_engine("act", e)

        @block.vector
        def _(e):
            run_engine("dve", e)

        @block.gpsimd
        def _(e):
            run_engine("pool", e)

        @block.sync
        def _(e):
            run_engine("sp", e)


class _Stop(Exception):
    pass


def build(debug=False, upto=None):
    nc = bass.Bass("TRN2", target_bir_lowering=False)

    def dram(name, shape, dt=F32, out=False):
        return nc.dram_tensor(name, shape, dt, kind="ExternalOutput" if out else "ExternalInput").ap()

    xh = dram("xh", [4096, 1024])
    w_in = dram("w_in", [1024, 5120])
    w_att = dram("w_att", [256, 1024])
    w_grp = dram("w_grp", [768, 192])
    w_po = dram("w_po", [768, 1024])
    w_out = dram("w_out", [1024, 1024])
    w_m1 = dram("w_m1", [1024, 4096])
    w_m2 = dram("w_m2", [4096, 1024])
    gmix_d = dram("gmix_b", [128, 1024])
    gmlp_d = dram("gmlp_b", [128, 1024])
    gfin_d = dram("gfin_b", [128, 1024])
    pscale_d = dram("pscale", [128, 6])
    btab_d = dram("btab", [128, 12 * 256])
    hbias_d = dram("hbias", [128, 1])
    ptab_d = dram("ptab", [128, 4 * 2 * 128])
    ptab0_d = dram("ptab0", [128, 4 * 128])
    ident_d = dram("ident", [128, 128])
    y = dram("y", [TOK, 1024], out=True)
    dbg = {}

    P = Prog()
    es = ExitStack()
    with es:
        arena = es.enter_context(nc.sbuf_tensor("arena", [128, ARENA_KB * 256], F32))
        psf = [es.enter_context(nc.psum_tensor("psf%d" % i, [128, 512], F32)) for i in range(8)]

        def V(off, nbytes, dt=F32, pat=None, **kw):
            assert off % 4 == 0 and nbytes % 4 == 0 and off + nbytes <= ARENA_KB * KB, (off, nbytes)
            a = arena[:, off // 4:(off + nbytes) // 4]
            if dt == BF16:
                a = a.bitcast(BF16)
            if pat:
                a = a.rearrange(pat, **kw)
            return a

        uT = V(0, 32 * KB, BF16, "p (k t) -> p k t", k=8)
        WS_OFF = 32 * KB
        ident_f = V(64 * KB, 512)
        ident_b = V(64 * KB + 512, 256, BF16)
        small = V(64 * KB + 768, 1024)
        eps_c = small[:, 0:1]
        hbias = small[:, 1:2]
        ss = small[:, 32:64]
        rs = small[:, 64:96]
        rstd = small[:, 96:128]
        ss2 = small[:, 128:144]
        rs2 = small[:, 144:160]
        rstd2 = small[:, 160:176]
        ss3 = small[:, 176:192]
        rs3 = small[:, 192:208]
        rstd3 = small[:, 208:224]
        uThl = V(66 * KB, 2 * KB, BF16, "p (k t) -> p k t", k=8)
        PB = 68 * KB

        w_in_v = w_in.rearrange("(k p) n -> p k n", p=128)
        w_att_v = w_att.rearrange("(k p) n -> p k n", p=128)
        w_po_v = w_po.rearrange("(k p) n -> p k n", p=128)
        w_grp_v = w_grp.rearrange("(g k p) n -> p g k n", g=4, k=2, p=96)
        w_out_v = w_out.rearrange("(k p) n -> p k n", p=128)
        w_m1_v = w_m1.rearrange("(k p) n -> p k n", p=128)
        w_m2_v = w_m2.rearrange("(k p) n -> p k n", p=128)
        S3_OFF = 106 * KB

        wl = {"sched": [], "next": 0, "res": {}, "ctr": 0, "phase": 0, "last": -1, "auto": True}

        def wl_plan(key, phase, fn, early=True, slot=None):
            wl["sched"].append((key, phase, fn, early, slot))

        def wl_issue():
            key, phase, fn, early, slot = wl["sched"][wl["next"]]
            wl["next"] += 1
            if slot is None:
                sl = wl["ctr"] % 2
                wl["ctr"] += 1
            else:
                sl = slot()
            base = S3_OFF if sl == 2 else WS_OFF + sl * 16 * KB
            wl["res"][key] = (sl, fn(base, sl))

        def wl_prefetch():
            if wl["next"] >= len(wl["sched"]) or wl["next"] > wl["last"] + 1:
                return
            key, phase, fn, early, slot = wl["sched"][wl["next"]]
            if phase == wl["phase"] or (phase == wl["phase"] + 1 and early):
                wl_issue()

        def wl_get(key):
            i = [k_[0] for k_ in wl["sched"]].index(key)
            while wl["next"] <= i:
                wl_issue()
            wl["last"] = i
            if wl["auto"]:
                wl_prefetch()
            return wl["res"][key]

        def wd(out, in_, sl):
            P.add("pool", I("dma_start", out=out, in_=in_), writes=[("ws", sl), ("ws", sl, "kv")],
                  dma_key=("wsd", sl))

        G3_OFF = WS_OFF + 20 * KB

        def ld_B(gi):
            def fn(base, sl):
                if gi == 2:
                    wsB = V(G3_OFF, 12 * KB, BF16, "p (s k n) -> p s k n", s=3, k=8)
                    wd(wsB[:, 0], w_in_v[:, :, 256 * gi:256 * gi + 256], sl)
                    return wsB
                wsB = V(base, 12 * KB, BF16, "p (s k n) -> p s k n", s=3, k=8)
                for s_i, col in enumerate([256 * gi, 768 + 256 * gi, 1536 + 256 * gi]):
                    wd(wsB[:, s_i], w_in_v[:, :, col:col + 256], sl)
                return wsB
            return fn

        def ld_C1(base, sl):
            wpz = V(base, 12 * KB, BF16, "p (k n) -> p k n", k=8)
            wgr = V(base + 12 * KB, 3 * KB, BF16, "p (g k n) -> p g k n", g=4, k=2)
            wd(wpz, w_in_v[:, :, 2304:3072], sl)
            wd(wgr[0:96], w_grp_v, sl)
            return (wpz, wgr)

        def ld_C2a(jp):
            def fn(base, sl):
                wga = V(base, 4 * KB, BF16, "p (k n) -> p k n", k=8)
                wgp = V(base + 4 * KB, 4 * KB, BF16, "p (k n) -> p k n", k=8)
                wat = V(base + 8 * KB, 1 * KB, BF16, "p (k n) -> p k n", k=2)
                wpo = V(base + 9 * KB, 3 * KB, BF16, "p (k n) -> p k n", k=6)
                c0 = 256 * jp
                wd(wga, w_in_v[:, :, 3072 + c0:3072 + c0 + 256], sl)
                wd(wgp, w_in_v[:, :, 4096 + c0:4096 + c0 + 256], sl)
                wd(wat, w_att_v[:, :, c0:c0 + 256], sl)
                wd(wpo, w_po_v[:, :, c0:c0 + 256], sl)
                return (wga, wgp, wat, wpo)
            return fn

        def ld_C2b(base, sl):
            wo = V(base, 16 * KB, BF16, "p (k n) -> p k n", k=8)
            wd(wo, w_out_v, sl)
            return wo

        def ld_D(fg):
            def fn(base, sl):
                w1 = V(base, 8 * KB, BF16, "p (k n) -> p k n", k=8)
                w2 = V(base + 8 * KB, 8 * KB, BF16, "p (k n) -> p k n", k=4)
                wd(w1, w_m1_v[:, :, fg * 512:(fg + 1) * 512], sl)
                wd(w2, w_m2_v[:, 4 * fg:4 * fg + 4, :], sl)
                return (w1, w2)
            return fn

        wl_plan(("B", 2), 1, ld_B(2), slot=lambda: 1)
        for gi_ in (1, 0):
            wl_plan(("B", gi_), 1, ld_B(gi_))
        wl_plan(("C1",), 2, ld_C1)
        for jp_ in range(4):
            wl_plan(("C2a", jp_), 3, ld_C2a(jp_))
        wl_plan(("C2b",), 4, ld_C2b)
        dslots = {}

        def d_slot(fg):
            def f():
                s_wo = wl["res"][("C2b",)][0]
                return [1 - s_wo, s_wo, 2][fg % 3]
            return f
        for fg_ in range(8):
            wl_plan(("D", fg_), 5, ld_D(fg_), early=(fg_ == 0), slot=d_slot(fg_))

        pools = {"all": [0, 1, 2, 3, 4, 5, 6, 7], "s": [0, 1, 2], "pv": [3, 4]}
        pool_ctr = {k: 0 for k in pools}

        def bank(pool="all"):
            i = pools[pool][pool_ctr[pool] % len(pools[pool])]
            pool_ctr[pool] += 1
            return i

        rr = [0]

        def evac_eng():
            rr[0] += 1
            return "act" if rr[0] % 2 else "dve"

        def copy_op(eng, out, in_, reads, writes, scale=None):
            if eng == "act":
                if scale is None:
                    P.add("act", I("activation", out=out, in_=in_, func=AF.Copy), reads, writes)
                else:
                    P.add("act", I("activation", out=out, in_=in_, func=AF.Copy, scale=scale), reads, writes)
            else:
                if scale is None:
                    P.add("dve", I("tensor_copy", out=out, in_=in_), reads, writes)
                else:
                    P.add("dve", I("tensor_scalar", out=out, in0=in_, scalar1=scale, scalar2=None,
                                                           op0=ALU.mult), reads, writes)

        def mm(out, lhsT, rhs, start, stop, reads, writes):
            P.add("pe", I("matmul", out, lhsT=lhsT, rhs=rhs, start=start, stop=stop), reads, writes)

        def dense(out_fn, pairs, reads, pool="all"):
            b = bank(pool)
            o = out_fn(psf[b])
            n = len(pairs)
            for i, (l, r) in enumerate(pairs):
                mm(o, l, r, i == 0, i == n - 1, reads, [("ps", b)])
            return b

        def wdma(out, in_, keys, dkey):
            P.add("pool", I("dma_start", out=out, in_=in_), writes=keys, dma_key=dkey)

        def dump(name, ap, shape, dt, reads):
            if not debug:
                return
            t = dram("dbg_" + name, shape, dt, out=True)
            dbg[name] = t
            P.add("sp", I("dma_start", out=t, in_=ap), reads=reads, dma_key="dbg_" + name, is_out=True)

        try:
            P.add("sp", I("dma_start", out=ident_f, in_=ident_d), writes=["ident_f"], dma_key="c0")
            P.add("pool", I("dma_start", out=ident_b, in_=ident_d), writes=["ident_b"], dma_key="c1")
            P.add("sp", I("dma_start", out=hbias, in_=hbias_d), writes=["hbias"], dma_key="c2")
            P.add("dve", I("memset", eps_c, 1e-6), writes=["eps"])

            o = PB
            xs = [V(o + i * 4 * KB, 4 * KB) for i in range(12)]; o += 48 * KB
            t_order = [4, 5, 0, 1, 2, 3, 6, 7]
            xs_base = {t_: (i_ % 3) * 4 for i_, t_ in enumerate(t_order)}
            xn = [V(o + i * 2 * KB, 2 * KB, BF16) for i in range(2)]; o += 8 * KB
            junk = V(o, 2 * KB, BF16); o += 2 * KB
            uTh = [V(o + i * 8 * KB, 8 * KB, BF16, "p (k t) -> p k t", k=8) for i in range(2)]; o += 16 * KB
            gmix_b = V(o, 4 * KB); o += 4 * KB
            TOP = 162 * KB
            KT = V(TOP, 16 * KB, BF16, "p (c t) -> p c t", c=2)
            VT = V(TOP + 16 * KB, 16 * KB, BF16, "p (c t) -> p c t", c=2)
            sK2 = V(TOP + 32 * KB, 2 * KB, BF16, "p (c t) -> p c t", c=2)
            sV2 = V(TOP + 34 * KB, 2 * KB, BF16, "p (c t) -> p c t", c=2)
            sK1 = V(TOP + 36 * KB, 512, BF16, "p (c t) -> p c t", c=2)
            sV1 = V(TOP + 36 * KB + 512, 512, BF16, "p (c t) -> p c t", c=2)
            aT = V(199 * KB, 8 * KB, BF16, "p (c t) -> p c t", c=2)

            P.add("sp", I("dma_start", out=gmix_b, in_=gmix_d), writes=["gmix_b"], dma_key="c3")
            wsA2 = V(WS_OFF, 24 * KB, BF16, "p (s k n) -> p s k n", s=2, k=8)
            for s_i, col in enumerate([768, 1536]):
                wdma(wsA2[:, s_i], w_in_v[:, :, col:col + 768],
                     [("ws", 0), ("ws", 1), ("ws", 0, "kv"), ("ws", 1, "kv")], "ws")
            wsB3 = V(G3_OFF, 12 * KB, BF16, "p (s k n) -> p s k n", s=3, k=8)

            def wsA_cols(s_i, kc, c):
                g_off = {0: 512, 1: 512, 2: 256, 3: 256, 4: 0, 5: 0}[s_i]
                return wsA2[:, s_i % 2, kc, g_off + c * 128:g_off + (c + 1) * 128]

            def norm_tile(t, src_rows_fn, xs_l, xn_l, ss_t, rs_t, rstd_t, g_b, dst_fn, load=True, tag="x"):
                for s_ in range(4):
                    gs = 4 * t + s_
                    sl = xs_base[gs // 4] + gs % 4
                    if load:
                        P.add("sp", I("dma_start", out=xs_l[sl], in_=src_rows_fn(gs)),
                              writes=[(tag + "s", sl)], dma_key=(tag + "s", sl))
                    P.add("act", I("activation", out=junk, in_=xs_l[sl], func=AF.Square,
                                                                      accum_out=ss_t[:, gs:gs + 1]),
                          reads=[(tag + "s", sl)], writes=["junk", (tag + "ss", gs)])
                    if t == t_order[0]:
                        P.add("act", I("activation", out=rs_t[:, gs:gs + 1], in_=ss_t[:, gs:gs + 1],
                                       func=AF.Sqrt, scale=1.0 / 1024.0, bias=eps_c),
                              reads=[(tag + "ss", gs), "eps"], writes=[(tag + "rsf", gs)])
                if t != t_order[0]:
                    P.add("act", I("activation", out=rs_t[:, 4 * t:4 * t + 4], in_=ss_t[:, 4 * t:4 * t + 4],
                                   func=AF.Sqrt, scale=1.0 / 1024.0, bias=eps_c),
                          reads=[(tag + "ss", 4 * t + i) for i in range(4)] + ["eps"], writes=[(tag + "rs", t)])

            def xn_and_transpose(t, xs_l, xn_l, rstd_t, g_b, gkey, dst_fn, dst_keys_fn, tag="x", rs_t=None, filler=None):
                rs_t = rs if rs_t is None else rs_t
                first = (t == t_order[0])
                if not first:
                    P.add("dve", I("reciprocal", out=rstd_t[:, 4 * t:4 * t + 4], in_=rs_t[:, 4 * t:4 * t + 4]),
                          reads=[(tag + "rs", t)], writes=[(tag + "rstd", t)])

                def stt(s_):
                    gs = 4 * t + s_
                    if first:
                        P.add("dve", I("reciprocal", out=rstd_t[:, gs:gs + 1], in_=rs_t[:, gs:gs + 1]),
                              reads=[(tag + "rsf", gs)], writes=[(tag + "rstd", t)])
                    sl = xs_base[gs // 4] + gs % 4
                    s2 = gs % 2
                    P.add("dve", I("scalar_tensor_tensor",
                        out=xn_l[s2], in0=xs_l[sl], scalar=rstd_t[:, gs:gs + 1], in1=g_b, op0=ALU.mult, op1=ALU.mult),
                        reads=[(tag + "s", sl), (tag + "rstd", t), gkey], writes=[(tag + "n", s2)])

                stt(0)
                for s_ in range(4):
                    gs = 4 * t + s_
                    s2 = gs % 2
                    if filler is not None:
                        filler(s_)
                    b = bank("all")
                    pbv = psf[b][:, :].bitcast(BF16)
                    for kc in range(8):
                        P.add("pe", I("transpose",
                            pbv[:, kc * 128:(kc + 1) * 128], xn_l[s2][:, kc * 128:(kc + 1) * 128], ident_b),
                            reads=[(tag + "n", s2), "ident_b"], writes=[("ps", b)])
                    if s_ + 1 < 4:
                        stt(s_ + 1)
                    copy_op("act" if s_ % 2 == 0 else "dve", dst_fn(gs),
                            pbv[:, 0:1024].rearrange("p (c t) -> p c t", c=8), [("ps", b)], dst_keys_fn(gs))

            def phaseA_norm(t):
                norm_tile(t, lambda gs: xh[gs * 128:(gs + 1) * 128, :], xs, xn, ss, rs, rstd, gmix_b, None)

            kv_pending = []

            def kv_filler(s_):
                n_ = (len(kv_pending) + (3 - s_)) // (4 - s_)
                for _ in range(n_):
                    kv_pending.pop(0)()

            def phaseA_rest(t):
                if t < 4:
                    hs = t % 2
                    xn_and_transpose(t, xs, xn, rstd, gmix_b, "gmix_b",
                                     lambda gs, hs=hs: uTh[hs][:, 0:8, (gs % 4) * 128:(gs % 4 + 1) * 128],
                                     lambda gs, hs=hs: [("uTh", hs)], filler=kv_filler)
                    jobs = [(0, KT, 512 * t, 512, "KT"), (1, VT, 512 * t, 512, "VT")]
                    if t == 3:
                        jobs += [(2, sK2, 0, 512, "sK2"), (3, sV2, 0, 512, "sV2"),
                                 (4, sK1, 0, 128, "sK1"), (5, sV1, 0, 128, "sV1")]
                    for (s_i, dst, off, n, key) in jobs:
                        for c in range(2):
                            def grp(s_i=s_i, dst=dst, off=off, n=n, key=key, c=c, hs=hs):
                                rhs0 = 512 - n
                                b = dense(lambda ps: ps[:, 0:n],
                                          [(wsA_cols(s_i, kc, c), uTh[hs][:, kc, rhs0:512]) for kc in range(8)],
                                          [("uTh", hs), ("ws", 0), ("ws", 1)])
                                copy_op(evac_eng(), dst[:, c, off:off + n], psf[b][:, 0:n], [("ps", b)], [(key, off // 512)])
                            kv_pending.append(grp)
                    if t == 3:
                        copy_op("act", uThl, uTh[hs][:, :, 384:512], [("uTh", hs)], ["uThl"])
                else:
                    T = t - 4
                    xn_and_transpose(t, xs, xn, rstd, gmix_b, "gmix_b",
                                     lambda gs: uT[:, 0:8, (gs - 16) * 128:(gs - 15) * 128],
                                     lambda gs, T=T: [("uT", T)], filler=kv_filler)

            phaseA_norm(t_order[0])
            for i_, t in enumerate(t_order):
                if i_ + 1 < 8:
                    phaseA_norm(t_order[i_ + 1])
                phaseA_rest(t)
                if t == 0:
                    P.add("pool", I("dma_start", out=wsB3[:, 1], in_=w_in_v[:, :, 768 + 512:768 + 768]),
                          reads=[("uTh", 0)], writes=[("wsB3", "k")], dma_key="wsB3k")
                    P.add("pool", I("dma_start", out=wsB3[:, 2], in_=w_in_v[:, :, 1536 + 512:1536 + 768]),
                          reads=[("uTh", 0)], writes=[("wsB3", "v")], dma_key="wsB3v")
                if t == 6:
                    assert not kv_pending
                    wl_prefetch()
            while kv_pending:
                kv_pending.pop(0)()
            dump("uT", uT, [128, 8, 2048], BF16, [("uT", i) for i in range(4)])
            wl_prefetch()
            P.barrier()
            if upto == 1:
                raise _Stop()
            wl["phase"] = 1

            pools["all"] = [5, 6, 7]
            o = PB
            btab = V(o, 4 * KB, F32, "p (h q) -> p h q", h=4); o += 4 * KB
            Vp = V(o, 32 * KB, BF16, "p (s h n) -> p s h n", s=32, h=4); o += 32 * KB
            QT2 = V(o, 16 * KB, BF16, "p (c h t) -> p c h t", c=2, h=2); o += 16 * KB
            acc = V(o, 32 * KB, F32, "p (h t) -> p h t", h=4); o += 32 * KB
            NST = 2
            sT = [V(o + i * 2 * KB, 2 * KB) for i in range(NST)]; o += NST * 2 * KB
            NPT = 6
            pTl = [V(o + i * KB, KB, BF16) for i in range(NPT)]; o += NPT * KB
            lnd = V(PB + 36 * KB, 8 * KB)
            for q_ in range(2):
                P.add("pool", I("memset", V(PB + 36 * KB + q_ * 8 * KB, 8 * KB), 0.0),
                      writes=["QT2z"] + [("QT", i) for i in range(4)])
            assert o <= TOP
            Vp_flat = V(PB + 4 * KB, 32 * KB, BF16, "p (s n) -> p s n", n=128)
            import os
            KSKIP = os.environ.get("KSKIP", "").split(",")
            if upto == 20:
                raise _Stop()
            if "memset" not in KSKIP:
                ONE2 = float(np.frombuffer(np.uint32(0x3F803F80).tobytes(), dtype=np.float32)[0])
                for q_ in range(4):
                    P.add("pool", I("memset", V(PB + 4 * KB + q_ * 8 * KB, 8 * KB), ONE2),
                          writes=["Vp_ones"] + [("Vp", i) for i in range(32)])
            KVALL = [("KT", i) for i in range(8)]
            VTALL = [("VT", i) for i in range(8)]
            ws_slot_ctr = [0]

            def ws_slot():
                s = ws_slot_ctr[0] % 2
                ws_slot_ctr[0] += 1
                return s

            sctr = [0]
            pctr = [0]
            first_group = True
            for gi in (2, 1, 0):
                d = (1, 4, 16)[gi]
                nb = 16 // d
                hl = 128 * d
                sl, wsB = wl_get(("B", gi))
                P.add("sp", I("dma_start", out=btab, in_=btab_d[:, gi * 1024:(gi + 1) * 1024].rearrange(
                    "p (h q) -> p h q", h=4)), writes=["btab"], dma_key="btab")
                if gi == 1:
                    copy_op("act", KT[:, :, 2048 - 512:2048], sK2, [("sK2", 0)] + KVALL, [("KT", 3)])
                    copy_op("dve", VT[:, :, 2048 - 512:2048], sV2, [("sV2", 0)] + VTALL, [("VT", 3)])
                if gi == 0:
                    copy_op("act", KT[:, :, 2048 - 128:2048], sK1, [("sK1", 0)] + KVALL, [("KT", 3)])
                    copy_op("dve", VT[:, :, 2048 - 128:2048], sV1, [("sV1", 0)] + VTALL, [("VT", 3)])
                def qkv_group(s_i, c, T):
                    if gi == 2 and s_i > 0:
                        wkeys = [("wsB3", "k" if s_i == 1 else "v"), ("ws", sl, "kv")]
                    else:
                        wkeys = [("ws", sl)]
                    b = dense(lambda ps: ps[:, 0:512],
                              [(wsB[:, s_i, kc, c * 128:(c + 1) * 128], uT[:, kc, T * 512:(T + 1) * 512]) for kc in range(8)],
                              [("uT", T)] + wkeys)
                    if s_i == 0:
                        ev_q = evac_eng()
                        for hh in range(2):
                            copy_op(ev_q, QT2[hh * 64:(hh + 1) * 64, c, hh, T * 512:(T + 1) * 512],
                                    psf[b][hh * 64:(hh + 1) * 64, 0:512], [("ps", b), "QT2z"], [("QT", T)], scale=0.125)
                    elif s_i == 1:
                        copy_op(evac_eng(), KT[:, c, 2048 + T * 512:2048 + (T + 1) * 512], psf[b][:, 0:512],
                                [("ps", b)], [("KT", 4 + T)])
                    else:
                        copy_op(evac_eng(), VT[:, c, 2048 + T * 512:2048 + (T + 1) * 512], psf[b][:, 0:512],
                                [("ps", b)], [("VT", 4 + T)])

                def vp_bank(r, c, js):
                    t0 = 2048 - hl + r
                    b = bank("all")
                    ev_e = evac_eng()
                    for qi, j in enumerate(js):
                        tj = t0 + j * 128 * d
                        mm(psf[b][:, qi * 128:(qi + 1) * 128], VT[:, c, tj:tj + 127 * d + 1:d], ident_b,
                           True, True, VTALL + ["ident_b"], [("ps", b)])
                    st0 = r * (nb + 1) + js[0]
                    ns_ = len(js)
                    copy_op(ev_e, Vp[:, st0:st0 + ns_, 2 * c:2 * c + 2, 0:64],
                            psf[b][:, 0:ns_ * 128].rearrange("p (s h n) -> p s h n", s=ns_, h=2),
                            [("ps", b)], [("Vp", st0 + i_) for i_ in range(ns_)])

                pools["all"] = [5, 6, 7, 0, 1, 2]
                for c in range(2):
                    for T in range(4):
                        qkv_group(2, c, T)
                vp_jobs = []
                for r in range(d):
                    for c in range(2):
                        for j0 in range(0, nb + 1, 4):
                            vp_jobs.append((r, c, list(range(j0, min(j0 + 4, nb + 1)))))
                qk_jobs = [(s_i, c, T) for s_i in (0, 1) for c in range(2) for T in range(4)]
                for i_, (s_i, c, T) in enumerate(qk_jobs):
                    qkv_group(s_i, c, T)
                    n_ = (len(vp_jobs) + (len(qk_jobs) - 1 - i_)) // (len(qk_jobs) - i_)
                    for _ in range(n_):
                        vp_bank(*vp_jobs.pop(0))
                assert not vp_jobs
                pools["all"] = [5, 6, 7]
                if upto == 22 + 10 * (2 - gi):
                    raise _Stop()
                QALL = [("QT", i) for i in range(4)] + ["QT2z"]
                ACCALL = [("acc", i) for i in range(4)]
                events = []
                for r in range(d):
                    for c in range(2):
                        for j in range(nb + 1):
                            qlo, qhi = max(j, 1), min(j + 1, nb)
                            events.append(("S", r, c, j, qlo, qhi))
                            if j >= 1:
                                events.append(("PV", r, 2 * c, j))
                                events.append(("PV", r, 2 * c + 1, j))
                CH = 2
                s_evs = [ev for ev in events if ev[0] == "S"]
                s_pos = {(ev[1], ev[2], ev[3]): i for i, ev in enumerate(s_evs)}
                pv_by_chunk = {}
                for ev in events:
                    if ev[0] == "PV":
                        pv_by_chunk.setdefault(s_pos[(ev[1], ev[2] // 2, ev[3])] // CH, []).append(ev)
                outl = []
                nchunks = (len(s_evs) + CH - 1) // CH
                for k_ in range(nchunks + 1):
                    if k_ < nchunks:
                        outl += s_evs[k_ * CH:(k_ + 1) * CH]
                    if k_ >= 1:
                        outl += pv_by_chunk.get(k_ - 1, [])
                ptile = {}
                pv_states = {}

                def flush_pv(key):
                    st_ = pv_states.get(key)
                    if st_ is None or st_["bank"] is None:
                        return
                    b = st_["bank"]
                    jobs = st_["jobs"]
                    if gi == 2:
                        r = jobs[0][0]
                        dst = acc[:, 0:4, r:r + 127 * 16 + 1:16]
                        src = psf[b][:, :].rearrange("p (h t) -> p h t", h=4)
                        P.add("act", I("activation", out=dst, in_=src, func=AF.Copy),
                              reads=[("ps", b)], writes=ACCALL)
                    else:
                        (r, h, m0) = jobs[0]
                        if gi == 1:
                            dst = acc[:, h, r:r + 511 * 4 + 1:4]
                        else:
                            dst = acc[:, h, (m0 - 1) * 128:(m0 - 1) * 128 + 512]
                        src = psf[b][:, 0:512]
                        P.add("dve", I("tensor_tensor", out=dst, in0=src, in1=dst, op=ALU.add),
                              reads=[("ps", b)] + ACCALL, writes=ACCALL)
                    st_["bank"] = None
                    st_["jobs"] = []

                for ev in outl:
                    if ev[0] == "S":
                        _, r, c, j, qlo, qhi = ev
                        n = 128 * (qhi - qlo + 1)
                        col0 = 0 if qlo == j else 128
                        kt0 = 2048 - hl + j * hl + r
                        q0 = (qlo - 1) * hl + r
                        b = bank("s")
                        si = sctr[0] % NST
                        sctr[0] += 1
                        pi = pctr[0] % NPT
                        pctr[0] += 1
                        ptile[(r, c, j)] = (pi, qlo, n)
                        mm(psf[b][:, 0:2 * n], KT[:, c, kt0:kt0 + 127 * d + 1:d],
                           QT2[:, c, :, q0:q0 + (n - 1) * d + 1:d], True, True, KVALL + QALL, [("ps", b)])
                        P.add("dve", I("tensor_tensor",
                            out=sT[si][:, 0:2 * n].rearrange("p (h q) -> p h q", h=2),
                            in0=psf[b][:, 0:2 * n].rearrange("p (h q) -> p h q", h=2),
                            in1=btab[:, 2 * c:2 * c + 2, col0:col0 + n], op=ALU.add),
                            reads=[("ps", b), "btab"], writes=[("sT", si)])
                        if j == 0:
                            P.add("act", I("activation",
                                out=pTl[pi][:, 0:2 * n], in_=sT[si][:, 0:2 * n], func=AF.Exp, bias=hbias),
                                reads=[("sT", si), "hbias"], writes=[("pT", pi)])
                        else:
                            P.add("act", I("activation",
                                out=pTl[pi][:, 0:2 * n], in_=sT[si][:, 0:2 * n], func=AF.Exp),
                                reads=[("sT", si)], writes=[("pT", pi)])
                    else:
                        _, r, h, m = ev
                        c, hh = h // 2, h % 2
                        key = 0 if gi == 2 else hh
                        st_ = pv_states.setdefault(key, {"bank": None, "jobs": []})
                        if st_["bank"] is None:
                            st_["bank"] = bank("pv")
                        b = st_["bank"]
                        q4 = len(st_["jobs"])
                        st_["jobs"].append((r, h, m))
                        st_prev = r * (nb + 1) + (m - 1)
                        st_cur = r * (nb + 1) + m
                        pi0, qlo0, n0 = ptile[(r, c, m - 1)]
                        pi1, qlo1, n1 = ptile[(r, c, m)]
                        o_ap = psf[b][:, q4 * 128:(q4 + 1) * 128]
                        a0 = hh * n0 + (m - qlo0) * 128
                        a1 = hh * n1 + (m - qlo1) * 128
                        mm(o_ap, Vp[:, st_prev, h, :], pTl[pi0][:, a0:a0 + 128], True, False,
                           [("Vp", st_prev), "Vp_ones", ("pT", pi0)], [("ps", b)])
                        mm(o_ap, Vp[:, st_cur, h, :], pTl[pi1][:, a1:a1 + 128], False, True,
                           [("Vp", st_cur), "Vp_ones", ("pT", pi1)], [("ps", b)])
                        if len(st_["jobs"]) == 4:
                            flush_pv(key)
                for key in list(pv_states):
                    flush_pv(key)
                if upto == 23 + (2 - gi):
                    raise _Stop()
            wl_prefetch()
            P.barrier()
            if upto == 2:
                raise _Stop()
            wl["phase"] = 2
            pools["all"] = [0, 1, 2, 3, 4, 5, 6, 7]
            ACCALL = [("acc", i) for i in range(4)]
            dump("acc", acc, [128, 4, 2048], F32, ACCALL)

            pT = V(PB, 24 * KB, BF16, "p (k t) -> p k t", k=6)
            class _Chunks:
                def __init__(self, views):
                    self.v = views

                def __getitem__(self, idx):
                    p_, ch_, t_ = idx
                    return self.v[ch_][p_, t_]

            pooled = _Chunks([V(a_ * KB, 4 * KB, BF16) for a_ in (92, 96, 100, 112, 116, 186, 190, 194)])
            o = 180 * KB
            ptab = V(o, 2 * KB, BF16, "p (g r t) -> p g r t", g=4, r=2); o += 2 * KB
            ptab0 = V(o, 1 * KB, BF16, "p (g t) -> p g t", g=4); o += 1 * KB
            pscale = V(o, 32); o += 32
            wpad = V(o, 2 * KB, BF16, "p (s a k n) -> p s a k n", s=2, a=2, k=2); o += 2 * KB
            wpad_f = V(o - 2 * KB, 2 * KB)
            pzk = V(152 * KB, 17 * 1536, BF16, "p (b n) -> p b n", n=768)
            sl, (wpz, wgr) = wl_get(("C1",))
            P.add("sp", I("dma_start", out=pscale[:, 0:6], in_=pscale_d), writes=["pscale"], dma_key="c4")
            P.add("pool", I("dma_start", out=ptab, in_=ptab_d.rearrange("p (g r t) -> p g r t", g=4, r=2)),
                  writes=["ptab"], dma_key="c5")
            P.add("pool", I("dma_start", out=ptab0, in_=ptab0_d.rearrange("p (g t) -> p g t", g=4)),
                  writes=["ptab0"], dma_key="c8")
            P.add("dve", I("memset", wpad_f, 0.0), writes=["wpad"])
            for s_ in range(2):
                P.add("pool", I("dma_start", out=wpad[0:96, s_, 0, :, 0:64], in_=w_grp_v[:, 2 * s_, :, 128:192]),
                      writes=["wpad"], dma_key="c9")
                P.add("pool", I("dma_start", out=wpad[0:96, s_, 1, :, 64:128], in_=w_grp_v[:, 2 * s_ + 1, :, 0:64]),
                      writes=["wpad"], dma_key="c9")
            for bl in range(17):
                for hf in range(2):
                    lhs = (lambda kc: uThl[:, kc, :]) if bl == 0 else (lambda kc, bl=bl: uT[:, kc, (bl - 1) * 128:bl * 128])
                    rd = ["uThl"] if bl == 0 else [("uT", (bl - 1) // 4)]
                    b = dense(lambda ps: ps[:, 0:384],
                              [(lhs(kc), wpz[:, kc, hf * 384:(hf + 1) * 384]) for kc in range(8)], rd + [("ws", sl)])
                    copy_op(evac_eng(), pzk[:, bl, hf * 384:(hf + 1) * 384], psf[b][:, 0:384], [("ps", b)], [("pzk", bl)])
                if bl in (3, 6, 9, 12):
                    h = bl // 3 - 1
                    c, pb = h // 2, (h % 2) * 64
                    P.add("act", I("activation", out=lnd[64:128, :], in_=acc[64:128, h, :], func=AF.Ln),
                          reads=ACCALL, writes=["lnd_hi"])
                    P.add("act", I("activation", out=lnd[0:64, :], in_=lnd[64:128, :], func=AF.Exp, scale=-1.0),
                          reads=["lnd_hi"], writes=["lnd_lo"])
                    P.add("dve", I("tensor_tensor",
                        out=aT[pb:pb + 64, c, :], in0=acc[0:64, h, :], in1=lnd[0:64, :], op=ALU.mult),
                        reads=ACCALL + ["lnd_lo"], writes=["aT"])
            dump("aT", aT, [128, 2, 2048], BF16, ["aT"])
            if upto == 3:
                raise _Stop()
            for g in range(4):
                for dc2 in range(2):
                    ch = 2 * g + dc2
                    for q0 in range(4):
                        b = bank("all")
                        for q in range(4):
                            bl = 1 + q0 * 4 + q
                            o_ap = psf[b][0:96, q * 128:(q + 1) * 128]
                            cur = ptab0[:, g, :] if bl == 1 else ptab[:, g, 1, :]
                            mm(o_ap, pzk[:, bl - 1, ch * 96:(ch + 1) * 96], ptab[:, g, 0, :], True, False,
                               [("pzk", bl - 1), "ptab"], [("ps", b)])
                            mm(o_ap, pzk[:, bl, ch * 96:(ch + 1) * 96], cur, False, True,
                               [("pzk", bl), "ptab", "ptab0"], [("ps", b)])
                        copy_op(evac_eng(), pooled[0:96, ch, q0 * 512:(q0 + 1) * 512], psf[b][0:96, 0:512],
                                [("ps", b)], [("pooled", ch, q0)])
                def pl_keys(gg, T):
                    return [("pooled", 2 * gg, T), ("pooled", 2 * gg + 1, T)]

                chunks = {0: [0], 1: [1, 2], 2: [3], 3: [4, 5]}[g]
                for ch6 in chunks:
                    for T in range(4):
                        ts = slice(T * 512, (T + 1) * 512)
                        if ch6 in (1, 4):
                            s_ = ch6 // 3
                            pairs = [(wpad[0:96, s_, 0, kc, :], pooled[0:96, 2 * (2 * s_) + kc, ts]) for kc in range(2)] + \
                                    [(wpad[0:96, s_, 1, kc, :], pooled[0:96, 2 * (2 * s_ + 1) + kc, ts]) for kc in range(2)]
                            rd = pl_keys(2 * s_, T) + pl_keys(2 * s_ + 1, T) + ["wpad"]
                        else:
                            gg = {0: 0, 2: 1, 3: 2, 5: 3}[ch6]
                            c_lo = 0 if ch6 in (0, 3) else 64
                            pairs = [(wgr[0:96, gg, kc, c_lo:c_lo + 128], pooled[0:96, 2 * gg + kc, ts]) for kc in range(2)]
                            rd = pl_keys(gg, T) + [("ws", sl)]
                        b = dense(lambda ps: ps[:, 0:512], pairs, rd)
                        P.add("act", I("activation",
                            out=pT[:, ch6, ts], in_=psf[b][:, 0:512], func=AF.Copy, scale=pscale[:, ch6:ch6 + 1]),
                            reads=[("ps", b), "pscale"], writes=["pT"])
            dump("pT", pT, [128, 6, 2048], BF16, ["pT"])
            wl_prefetch()
            P.barrier()
            if upto == 4:
                raise _Stop()
            wl["phase"] = 3

            o = PB + 32 * KB
            mergedT = V(o, 32 * KB, BF16, "p (k t) -> p k t", k=8); o += 32 * KB
            sg = [[V(o + (i * 4 + j) * 2 * KB, 2 * KB) for j in range(4)] for i in range(2)]; o += 16 * KB
            it = 0
            for jp in range(4):
                sl, (wga, wgp, wat, wpo) = wl_get(("C2a", jp))
                for je in range(2):
                    j = 2 * jp + je
                    cs = slice(je * 128, (je + 1) * 128)
                    for T in range(4):
                        ts = slice(T * 512, (T + 1) * 512)
                        bga = dense(lambda ps: ps[:, 0:512], [(wga[:, kc, cs], uT[:, kc, ts]) for kc in range(8)],
                                    [("uT", T), ("ws", sl)])
                        bgp = dense(lambda ps: ps[:, 0:512], [(wgp[:, kc, cs], uT[:, kc, ts]) for kc in range(8)],
                                    [("uT", T), ("ws", sl)])
                        ba = dense(lambda ps: ps[:, 0:512], [(wat[:, kc, cs], aT[:, kc, ts]) for kc in range(2)],
                                   ["aT", ("ws", sl)])
                        bp = dense(lambda ps: ps[:, 0:512], [(wpo[:, kc, cs], pT[:, kc, ts]) for kc in range(6)],
                                   ["pT", ("ws", sl)])
                        s_ = sg[it % 2]
                        ik = it % 2
                        it += 1
                        P.add("act", I("activation", out=s_[0], in_=psf[bga][:, 0:512], func=AF.Sigmoid),
                              reads=[("ps", bga)], writes=[("sg", ik, 0)])
                        P.add("act", I("activation", out=s_[1], in_=psf[bgp][:, 0:512], func=AF.Sigmoid),
                              reads=[("ps", bgp)], writes=[("sg", ik, 1)])
                        P.add("dve", I("tensor_tensor", out=s_[2], in0=psf[ba][:, 0:512], in1=s_[0], op=ALU.mult),
                              reads=[("ps", ba), ("sg", ik, 0)], writes=[("sg", ik, 2)])
                        P.add("dve", I("tensor_tensor", out=s_[3], in0=psf[bp][:, 0:512], in1=s_[1], op=ALU.mult),
                              reads=[("ps", bp), ("sg", ik, 1)], writes=[("sg", ik, 3)])
                        P.add("dve", I("tensor_tensor", out=mergedT[:, j, ts], in0=s_[2], in1=s_[3], op=ALU.add),
                              reads=[("sg", ik, 2), ("sg", ik, 3)], writes=[("mergedT", T)])
            dump("mergedT", mergedT, [128, 8, 2048], BF16, [("mergedT", i) for i in range(4)])
            wl_prefetch()
            P.barrier()
            if upto == 5:
                raise _Stop()
            wl["phase"] = 4

            hbuf = V(132 * KB, 64 * KB, F32, "p (s n) -> p s n", s=16)
            o = PB
            mn = [V(o + i * 2 * KB, 2 * KB, BF16) for i in range(2)]; o += 8 * KB
            gmlp_b = V(o, 4 * KB); o += 4 * KB
            junk = V(o, 2 * KB, BF16); o += 2 * KB
            P.add("sp", I("dma_start", out=gmlp_b, in_=gmlp_d), writes=["gmlp_b"], dma_key="c6")
            sl, wo = wl_get(("C2b",))
            for s_ in range(16):
                P.add("sp", I("dma_start", out=hbuf[:, s_, :], in_=xh[2048 + s_ * 128:2048 + (s_ + 1) * 128, :]),
                      writes=[("h", s_)], dma_key=("hld", s_))
            mT = uT
            def c2b_dense_sub(s_):
                T = s_ // 4
                for half in range(2):
                    es_ = slice(half * 512, (half + 1) * 512)
                    b = dense(lambda ps: ps[:, 0:512],
                              [(mergedT[:, kc, s_ * 128:(s_ + 1) * 128], wo[:, kc, es_]) for kc in range(8)],
                              [("mergedT", T), ("ws", sl)])
                    P.add("dve", I("tensor_tensor",
                        out=hbuf[:, s_, es_], in0=psf[b][:, 0:512], in1=hbuf[:, s_, es_], op=ALU.add),
                        reads=[("ps", b), ("h", s_)], writes=[("h", s_)])
                P.add("act", I("activation", out=junk, in_=hbuf[:, s_, :], func=AF.Square,
                               accum_out=ss2[:, s_:s_ + 1]),
                      reads=[("h", s_)], writes=["junk", ("ss2", s_)])
                if s_ % 4 == 3:
                    P.add("act", I("activation", out=rs2[:, 4 * T:4 * T + 4], in_=ss2[:, 4 * T:4 * T + 4],
                                   func=AF.Sqrt, scale=1.0 / 1024.0, bias=eps_c),
                          reads=[("ss2", 4 * T + i) for i in range(4)] + ["eps"], writes=[("rs2", T)])

            def c2b_stt(sq):
                T = sq // 4
                s2 = sq % 2
                if sq % 4 == 0:
                    P.add("dve", I("reciprocal", out=rstd2[:, 4 * T:4 * T + 4], in_=rs2[:, 4 * T:4 * T + 4]),
                          reads=[("rs2", T)], writes=[("rstd2", T)])
                P.add("dve", I("scalar_tensor_tensor",
                    out=mn[s2], in0=hbuf[:, sq, :], scalar=rstd2[:, sq:sq + 1], in1=gmlp_b, op0=ALU.mult, op1=ALU.mult),
                    reads=[("h", sq), ("rstd2", T), "gmlp_b"], writes=[("mn", s2)])

            def c2b_tr(sq):
                T = sq // 4
                s2 = sq % 2
                b = bank("all")
                pbv = psf[b][:, :].bitcast(BF16)
                for kc in range(8):
                    P.add("pe", I("transpose",
                        pbv[:, kc * 128:(kc + 1) * 128], mn[s2][:, kc * 128:(kc + 1) * 128], ident_b),
                        reads=[("mn", s2), "ident_b"], writes=[("ps", b)])
                return b, pbv

            def c2b_evac(sq, b, pbv):
                T = sq // 4
                copy_op("act" if sq % 2 == 0 else "dve", mT[:, 0:8, sq * 128:(sq + 1) * 128],
                        pbv[:, 0:1024].rearrange("p (c t) -> p c t", c=8), [("ps", b)], [("uT", T)])

            for s_ in range(4):
                c2b_dense_sub(s_)
            c2b_stt(0)
            for sq in range(16):
                if sq + 4 < 16:
                    c2b_dense_sub(sq + 4)
                b_, pbv_ = c2b_tr(sq)
                if sq + 1 < 16:
                    c2b_stt(sq + 1)
                c2b_evac(sq, b_, pbv_)
            dump("h", hbuf, [128, 16, 1024], F32, [("h", i) for i in range(16)])
            wl_prefetch()
            P.barrier()
            if upto == 6:
                raise _Stop()
            wl["phase"] = 5

            o = PB
            hidT = [V(o + i * 16 * KB, 16 * KB, BF16, "p (k t) -> p k t", k=4) for i in range(2)]; o += 32 * KB
            rtmp = [V(o + i * 2 * KB, 2 * KB) for i in range(2)]; o += 4 * KB
            junk = V(o, 2 * KB, BF16); o += 2 * KB
            assert o <= S3_OFF
            gfin_b = V(122 * KB, 4 * KB)
            P.add("sp", I("dma_start", out=gfin_b, in_=gfin_d), writes=["gfin_b"], dma_key="c7")
            NFG = 8
            wsl = {}
            wl["auto"] = False

            def load_fg(fg):
                sl, (w1, w2) = wl_get(("D", fg))
                wsl[fg] = (sl, w1, w2)

            rc = [0]

            def mlp_in(fg):
                sl, w1, w2 = wsl[fg]
                hT = hidT[fg % 2]
                for fc in range(4):
                    for T in range(4):
                        ts = slice(T * 512, (T + 1) * 512)
                        b = dense(lambda ps: ps[:, 0:512],
                                  [(w1[:, kc, fc * 128:(fc + 1) * 128], mT[:, kc, ts]) for kc in range(8)],
                                  [("uT", T), ("ws", sl)])
                        ri = rc[0] % 2
                        rc[0] += 1
                        P.add("act", I("activation", out=rtmp[ri], in_=psf[b][:, 0:512], func=AF.Relu),
                              reads=[("ps", b)], writes=[("rtmp", ri)])
                        P.add("dve", I("scalar_tensor_tensor",
                            out=hT[:, fc, ts], in0=psf[b][:, 0:512], scalar=0.0, in1=rtmp[ri], op0=ALU.max, op1=ALU.mult),
                            reads=[("ps", b), ("rtmp", ri)], writes=[("hidT", fg % 2, T)])

            def final_norm(T):
                for s_ in range(4 * T, 4 * T + 4):
                    P.add("act", I("activation", out=junk, in_=hbuf[:, s_, :], func=AF.Square,
                                   accum_out=ss3[:, s_:s_ + 1]),
                          reads=[("h", s_)], writes=["junk", ("ss3", s_)])
                P.add("act", I("activation", out=rs3[:, 4 * T:4 * T + 4], in_=ss3[:, 4 * T:4 * T + 4],
                               func=AF.Sqrt, scale=1.0 / 1024.0, bias=eps_c),
                      reads=[("ss3", 4 * T + i) for i in range(4)] + ["eps"], writes=[("rs3", T)])
                P.add("dve", I("reciprocal", out=rstd3[:, 4 * T:4 * T + 4], in_=rs3[:, 4 * T:4 * T + 4]),
                      reads=[("rs3", T)], writes=[("rstd3", T)])
                for sq in range(4 * T, 4 * T + 4):
                    P.add("dve", I("scalar_tensor_tensor",
                        out=hbuf[:, sq, :], in0=hbuf[:, sq, :], scalar=rstd3[:, sq:sq + 1], in1=gfin_b,
                        op0=ALU.mult, op1=ALU.mult),
                        reads=[("h", sq), ("rstd3", T), "gfin_b"], writes=[("h", sq)])
                    P.add("sp", I("dma_start", out=y[sq * 128:(sq + 1) * 128, :], in_=hbuf[:, sq, :]),
                          reads=[("h", sq)], dma_key=("yst", sq), is_out=True)

            def mlp_out(fg):
                sl, w1, w2 = wsl[fg]
                hT = hidT[fg % 2]
                for s_ in range(16):
                    T = s_ // 4
                    for half in range(2):
                        es_ = slice(half * 512, (half + 1) * 512)
                        b = dense(lambda ps: ps[:, 0:512],
                                  [(hT[:, fc, s_ * 128:(s_ + 1) * 128], w2[:, fc, es_]) for fc in range(4)],
                                  [("hidT", fg % 2, T), ("ws", sl)])
                        P.add("dve", I("tensor_tensor",
                            out=hbuf[:, s_, es_], in0=psf[b][:, 0:512], in1=hbuf[:, s_, es_], op=ALU.add),
                            reads=[("ps", b), ("h", s_)], writes=[("h", s_)])
                    if fg == NFG - 1 and s_ % 4 == 3 and T >= 1:
                        final_norm(T - 1)
                if fg == NFG - 1:
                    final_norm(3)

            load_fg(0)
            load_fg(1)
            load_fg(2)
            mlp_in(0)
            for fg in range(NFG):
                if fg + 1 < NFG:
                    mlp_in(fg + 1)
                mlp_out(fg)
                if fg + 3 < NFG:
                    load_fg(fg + 3)

        except _Stop:
            pass
        names = ["eng_" + e for e in Prog.ENGS] + ["dma_%d" % i for i in range(len(P.dma_keys))]
        sems = {n: es.enter_context(nc.semaphore(n)) for n in names}
        block = es.enter_context(nc.Block())
        P.emit(nc, block, sems)
    return nc, dbg


def host_inputs(x, norm_mix_g, w_in, w_att_out, w_pool_grp, pool_scale, w_pool_out, w_out,
                norm_mlp_g, w_mlp_in, w_mlp_out, norm_final_g):
    f = np.float32
    x2 = np.asarray(x, f)[0]
    common = {
        "w_in": np.ascontiguousarray(np.asarray(w_in, f)[0]),
        "w_att": np.ascontiguousarray(np.asarray(w_att_out, f)[0]),
        "w_grp": np.ascontiguousarray(np.asarray(w_pool_grp, f)[0].reshape(768, 192)),
        "w_po": np.ascontiguousarray(np.asarray(w_pool_out, f)[0]),
        "w_out": np.ascontiguousarray(np.asarray(w_out, f)[0]),
        "w_m1": np.ascontiguousarray(np.asarray(w_mlp_in, f)[0]),
        "w_m2": np.ascontiguousarray(np.asarray(w_mlp_out, f)[0]),
        "gmix_b": np.ascontiguousarray(np.broadcast_to(np.asarray(norm_mix_g, f)[0], (128, 1024))),
        "gmlp_b": np.ascontiguousarray(np.broadcast_to(np.asarray(norm_mlp_g, f)[0], (128, 1024))),
        "gfin_b": np.ascontiguousarray(np.broadcast_to(np.asarray(norm_final_g, f), (128, 1024))),
        "pscale": np.ascontiguousarray(np.asarray(pool_scale, f)[0].reshape(6, 128).T),
        "ident": np.eye(128, dtype=f),
    }
    slopes = 2.0 ** (-8.0 * (np.arange(12, dtype=np.float64) + 1.0) / 12.0)
    k_i = np.arange(128)[:, None]
    q_i = np.arange(256)[None, :]
    steps = q_i - k_i
    valid = (steps >= 0) & (steps <= 128)
    bt = np.zeros((128, 12, 256), f)
    for hd in range(12):
        dil = (1, 4, 16)[hd // 4]
        bt[:, hd, :] = np.where(valid, -(slopes[hd] * steps * dil), NEG).astype(f)
    common["btab"] = np.ascontiguousarray(bt.reshape(128, 12 * 256))
    tk = np.arange(128)[:, None]
    tq = np.arange(128)[None, :]
    pt = np.zeros((128, 4, 2, 128), f)
    for g, w in enumerate((2, 4, 8, 16)):
        dlt = tq - tk
        pt[:, g, 1, :] = np.where((dlt >= 0) & (dlt <= w - 1), 1.0 / w, 0.0) - (dlt == 0)
        dlp = tq + 128 - tk
        pt[:, g, 0, :] = np.where(dlp <= w - 1, 1.0 / w, 0.0)
    common["ptab"] = np.ascontiguousarray(pt.reshape(128, 1024))
    in_maps = []
    for c in range(NCORES):
        xhc = np.zeros((4096, 1024), f)
        if c > 0:
            xhc[0:2048] = x2[(c - 1) * TOK:c * TOK]
        xhc[2048:] = x2[c * TOK:(c + 1) * TOK]
        hb = np.full((128, 1), NEG if c == 0 else 0.0, f)
        p0 = np.zeros((128, 4, 128), f)
        for g, w in enumerate((2, 4, 8, 16)):
            dlt = tq - tk
            cnt = np.minimum(tq + 1, w) if c == 0 else w
            p0[:, g, :] = np.where((dlt >= 0) & (dlt <= w - 1), 1.0 / cnt, 0.0) - (dlt == 0)
        m = dict(common)
        m["xh"] = xhc
        m["hbias"] = hb
        m["ptab0"] = np.ascontiguousarray(p0.reshape(128, 512))
        in_maps.append(m)
    return in_maps


_CACHE = {}


def kernel(**inputs):
    in_maps = host_inputs(**inputs)
    if "nc" not in _CACHE:
        _CACHE["nc"] = build(False)[0]
    nc = _CACHE["nc"]
    res = run_bass_kernel_spmd(nc, in_maps, core_ids=list(range(NCORES)))
    out = np.concatenate([np.asarray(r["y"], np.float32) for r in res.results], axis=0)
    return out.reshape(1, NCORES * TOK, 1024)
```

```python
from contextlib import ExitStack
import numpy as np
import concourse.bass as bass
import concourse.mybir as mybir
from concourse.bass_utils import run_bass_kernel_spmd

F32 = mybir.dt.float32
BF16 = mybir.dt.bfloat16
ALU = mybir.AluOpType
AF = mybir.ActivationFunctionType

NCORES = 8
TOK = 2048
KB = 1024
ARENA_KB = 207
NEG = -30000.0


def I(name, *a, **k):
    return lambda e: getattr(e, name)(*a, **k)


class Op:
    __slots__ = ("eng", "fn", "deps", "dma_key", "seq", "has_dep", "idx", "is_out")


class Prog:
    ENGS = ("pe", "act", "dve", "pool", "sp")

    def __init__(self):
        self.ops = []
        self.last_w = {}
        self.readers = {}
        self.dma_keys = []
        self.last_eng = {}
        self.last_dma = {}
        self.bar_deps = set()

    def add(self, eng, fn, reads=(), writes=(), dma_key=None, is_out=False):
        op = Op()
        op.eng = eng
        op.fn = fn
        op.idx = len(self.ops)
        op.dma_key = dma_key
        op.has_dep = False
        op.seq = 0
        op.is_out = is_out
        deps = set(self.bar_deps)
        for k in reads:
            w = self.last_w.get(k)
            if w is not None:
                deps.add(w)
        for k in writes:
            w = self.last_w.get(k)
            if w is not None:
                deps.add(w)
            for r in self.readers.get(k, ()):
                deps.add(r)
        op.deps = deps
        for k in reads:
            self.readers.setdefault(k, []).append(op.idx)
        for k in writes:
            self.last_w[k] = op.idx
            self.readers[k] = []
        if dma_key is not None:
            if dma_key not in self.dma_keys:
                self.dma_keys.append(dma_key)
            self.last_dma[dma_key] = op.idx
        else:
            self.last_eng[eng] = op.idx
        self.ops.append(op)
        return op

    def barrier(self):
        self.bar_deps = set(self.last_eng.values()) | set(self.last_dma.values())

    def emit(self, nc, block, sems):
        ops = self.ops
        for op in ops:
            for d in op.deps:
                dop = ops[d]
                if dop.dma_key is None and not (dop.eng == "pe" and op.eng == "pe"):
                    dop.has_dep = True
        cnt = {e: 0 for e in self.ENGS}
        for op in ops:
            if op.dma_key is None and op.has_dep:
                cnt[op.eng] += 1
                op.seq = cnt[op.eng]
        dma_cum = {}
        dma_at = {}
        run = {}
        for op in ops:
            if op.dma_key is not None:
                run[op.dma_key] = run.get(op.dma_key, 0) + 16
                dma_cum.setdefault(op.dma_key, []).append((op.idx, run[op.dma_key]))
                dma_at[op.idx] = run[op.dma_key]
        eng_sem = {e: sems["eng_" + e] for e in self.ENGS}
        dma_sem = {k: sems["dma_%d" % i] for i, k in enumerate(self.dma_keys)}

        def cum_before(key, idx):
            v = 0
            for (i, c) in dma_cum[key]:
                if i < idx:
                    v = c
                else:
                    break
            return v

        def run_engine(eng, e):
            waited = {}
            for op in ops:
                if op.eng != eng:
                    continue
                for d in sorted(op.deps):
                    dop = ops[d]
                    if dop.dma_key is not None:
                        s = dma_sem[dop.dma_key]
                        val = dma_at[dop.idx]
                        key = ("d", dop.dma_key)
                    else:
                        if dop.eng == "pe" and eng == "pe":
                            continue
                        s = eng_sem[dop.eng]
                        val = dop.seq
                        key = ("e", dop.eng)
                    if waited.get(key, 0) >= val:
                        continue
                    e.wait_ge(s, val)
                    waited[key] = val
                inst = op.fn(e)
                if op.dma_key is not None:
                    inst.then_inc(dma_sem[op.dma_key], 16)
                elif op.has_dep:
                    inst.then_inc(eng_sem[eng], 1)
            if eng == "sp":
                for k in self.dma_keys:
                    if any(o.is_out for o in ops if o.dma_key == k):
                        e.wait_ge(dma_sem[k], dma_cum[k][-1][1])

        @block.tensor
        def _(e):
            run_engine("pe", e)

        @block.scalar
        def _(e):
            run_engine("act", e)

        @block.vector
        def _(e):
            run_engine("dve", e)

        @block.gpsimd
        def _(e):
            run_engine("pool", e)

        @block.sync
        def _(e):
            run_engine("sp", e)


class _Stop(Exception):
    pass


def build(debug=False, upto=None):
    nc = bass.Bass("TRN2", target_bir_lowering=False)

    def dram(name, shape, dt=F32, out=False):
        return nc.dram_tensor(name, shape, dt, kind="ExternalOutput" if out else "ExternalInput").ap()

    xh = dram("xh", [4096, 1024])
    w_in = dram("w_in", [1024, 5120])
    w_att = dram("w_att", [256, 1024])
    w_grp = dram("w_grp", [768, 192])
    w_po = dram("w_po", [768, 1024])
    w_out = dram("w_out", [1024, 1024])
    w_m1 = dram("w_m1", [1024, 4096])
    w_m2 = dram("w_m2", [4096, 1024])
    gmix_d = dram("gmix_b", [128, 1024])
    gmlp_d = dram("gmlp_b", [128, 1024])
    gfin_d = dram("gfin_b", [128, 1024])
    pscale_d = dram("pscale", [128, 6])
    btab_d = dram("btab", [128, 12 * 256])
    hbias_d = dram("hbias", [128, 1])
    ptab_d = dram("ptab", [128, 4 * 2 * 128])
    ptab0_d = dram("ptab0", [128, 4 * 128])
    ident_d = dram("ident", [128, 128])
    y = dram("y", [TOK, 1024], out=True)
    dbg = {}

    P = Prog()
    es = ExitStack()
    with es:
        arena = es.enter_context(nc.sbuf_tensor("arena", [128, ARENA_KB * 256], F32))
        psf = [es.enter_context(nc.psum_tensor("psf%d" % i, [128, 512], F32)) for i in range(8)]

        def V(off, nbytes, dt=F32, pat=None, **kw):
            assert off % 4 == 0 and nbytes % 4 == 0 and off + nbytes <= ARENA_KB * KB, (off, nbytes)
            a = arena[:, off // 4:(off + nbytes) // 4]
            if dt == BF16:
                a = a.bitcast(BF16)
            if pat:
                a = a.rearrange(pat, **kw)
            return a

        uT = V(0, 32 * KB, BF16, "p (k t) -> p k t", k=8)
        WS_OFF = 32 * KB
        ident_f = V(64 * KB, 512)
        ident_b = V(64 * KB + 512, 256, BF16)
        small = V(64 * KB + 768, 1024)
        eps_c = small[:, 0:1]
        hbias = small[:, 1:2]
        ss = small[:, 32:64]
        rs = small[:, 64:96]
        rstd = small[:, 96:128]
        ss2 = small[:, 128:144]
        rs2 = small[:, 144:160]
        rstd2 = small[:, 160:176]
        ss3 = small[:, 176:192]
        rs3 = small[:, 192:208]
        rstd3 = small[:, 208:224]
        uThl = V(66 * KB, 2 * KB, BF16, "p (k t) -> p k t", k=8)
        PB = 68 * KB

        w_in_v = w_in.rearrange("(k p) n -> p k n", p=128)
        w_att_v = w_att.rearrange("(k p) n -> p k n", p=128)
        w_po_v = w_po.rearrange("(k p) n -> p k n", p=128)
        w_grp_v = w_grp.rearrange("(g k p) n -> p g k n", g=4, k=2, p=96)
        w_out_v = w_out.rearrange("(k p) n -> p k n", p=128)
        w_m1_v = w_m1.rearrange("(k p) n -> p k n", p=128)
        w_m2_v = w_m2.rearrange("(k p) n -> p k n", p=128)
        S3_OFF = 106 * KB

        wl = {"sched": [], "next": 0, "res": {}, "ctr": 0, "phase": 0, "last": -1, "auto": True}

        def wl_plan(key, phase, fn, early=True, slot=None):
            wl["sched"].append((key, phase, fn, early, slot))

        def wl_issue():
            key, phase, fn, early, slot = wl["sched"][wl["next"]]
            wl["next"] += 1
            if slot is None:
                sl = wl["ctr"] % 2
                wl["ctr"] += 1
            else:
                sl = slot()
            base = S3_OFF if sl == 2 else WS_OFF + sl * 16 * KB
            wl["res"][key] = (sl, fn(base, sl))

        def wl_prefetch():
            if wl["next"] >= len(wl["sched"]) or wl["next"] > wl["last"] + 1:
                return
            key, phase, fn, early, slot = wl["sched"][wl["next"]]
            if phase == wl["phase"] or (phase == wl["phase"] + 1 and early):
                wl_issue()

        def wl_get(key):
            i = [k_[0] for k_ in wl["sched"]].index(key)
            while wl["next"] <= i:
                wl_issue()
            wl["last"] = i
            if wl["auto"]:
                wl_prefetch()
            return wl["res"][key]

        def wd(out, in_, sl):
            P.add("pool", I("dma_start", out=out, in_=in_), writes=[("ws", sl), ("ws", sl, "kv")],
                  dma_key=("wsd", sl))

        G3_OFF = WS_OFF + 20 * KB

        def ld_B(gi):
            def fn(base, sl):
                if gi == 2:
                    wsB = V(G3_OFF, 12 * KB, BF16, "p (s k n) -> p s k n", s=3, k=8)
                    wd(wsB[:, 0], w_in_v[:, :, 256 * gi:256 * gi + 256], sl)
                    return wsB
                wsB = V(base, 12 * KB, BF16, "p (s k n) -> p s k n", s=3, k=8)
                for s_i, col in enumerate([256 * gi, 768 + 256 * gi, 1536 + 256 * gi]):
                    wd(wsB[:, s_i], w_in_v[:, :, col:col + 256], sl)
                return wsB
            return fn

        def ld_C1(base, sl):
            wpz = V(base, 12 * KB, BF16, "p (k n) -> p k n", k=8)
            wgr = V(base + 12 * KB, 3 * KB, BF16, "p (g k n) -> p g k n", g=4, k=2)
            wd(wpz, w_in_v[:, :, 2304:3072], sl)
            wd(wgr[0:96], w_grp_v, sl)
            return (wpz, wgr)

        def ld_C2a(jp):
            def fn(base, sl):
                wga = V(base, 4 * KB, BF16, "p (k n) -> p k n", k=8)
                wgp = V(base + 4 * KB, 4 * KB, BF16, "p (k n) -> p k n", k=8)
                wat = V(base + 8 * KB, 1 * KB, BF16, "p (k n) -> p k n", k=2)
                wpo = V(base + 9 * KB, 3 * KB, BF16, "p (k n) -> p k n", k=6)
                c0 = 256 * jp
                wd(wga, w_in_v[:, :, 3072 + c0:3072 + c0 + 256], sl)
                wd(wgp, w_in_v[:, :, 4096 + c0:4096 + c0 + 256], sl)
                wd(wat, w_att_v[:, :, c0:c0 + 256], sl)
                wd(wpo, w_po_v[:, :, c0:c0 + 256], sl)
                return (wga, wgp, wat, wpo)
            return fn

        def ld_C2b(base, sl):
            wo = V(base, 16 * KB, BF16, "p (k n) -> p k n", k=8)
            wd(wo, w_out_v, sl)
            return wo

        def ld_D(fg):
            def fn(base, sl):
                w1 = V(base, 8 * KB, BF16, "p (k n) -> p k n", k=8)
                w2 = V(base + 8 * KB, 8 * KB, BF16, "p (k n) -> p k n", k=4)
                wd(w1, w_m1_v[:, :, fg * 512:(fg + 1) * 512], sl)
                wd(w2, w_m2_v[:, 4 * fg:4 * fg + 4, :], sl)
                return (w1, w2)
            return fn

        wl_plan(("B", 2), 1, ld_B(2), slot=lambda: 1)
        for gi_ in (1, 0):
            wl_plan(("B", gi_), 1, ld_B(gi_))
        wl_plan(("C1",), 2, ld_C1)
        for jp_ in range(4):
            wl_plan(("C2a", jp_), 3, ld_C2a(jp_))
        wl_plan(("C2b",), 4, ld_C2b)
        dslots = {}

        def d_slot(fg):
            def f():
                s_wo = wl["res"][("C2b",)][0]
                return [1 - s_wo, s_wo, 2][fg % 3]
            return f
        for fg_ in range(8):
            wl_plan(("D", fg_), 5, ld_D(fg_), early=(fg_ == 0), slot=d_slot(fg_))

        pools = {"all": [0, 1, 2, 3, 4, 5, 6, 7], "s": [0, 1, 2], "pv": [3, 4]}
        pool_ctr = {k: 0 for k in pools}

        def bank(pool="all"):
            i = pools[pool][pool_ctr[pool] % len(pools[pool])]
            pool_ctr[pool] += 1
            return i

        rr = [0]

        def evac_eng():
            rr[0] += 1
            return "act" if rr[0] % 2 else "dve"

        def copy_op(eng, out, in_, reads, writes, scale=None):
            if eng == "act":
                if scale is None:
                    P.add("act", I("activation", out=out, in_=in_, func=AF.Copy), reads, writes)
                else:
                    P.add("act", I("activation", out=out, in_=in_, func=AF.Copy, scale=scale), reads, writes)
            else:
                if scale is None:
                    P.add("dve", I("tensor_copy", out=out, in_=in_), reads, writes)
                else:
                    P.add("dve", I("tensor_scalar", out=out, in0=in_, scalar1=scale, scalar2=None,
                                                           op0=ALU.mult), reads, writes)

        def mm(out, lhsT, rhs, start, stop, reads, writes):
            P.add("pe", I("matmul", out, lhsT=lhsT, rhs=rhs, start=start, stop=stop), reads, writes)

        def dense(out_fn, pairs, reads, pool="all"):
            b = bank(pool)
            o = out_fn(psf[b])
            n = len(pairs)
            for i, (l, r) in enumerate(pairs):
                mm(o, l, r, i == 0, i == n - 1, reads, [("ps", b)])
            return b

        def wdma(out, in_, keys, dkey):
            P.add("pool", I("dma_start", out=out, in_=in_), writes=keys, dma_key=dkey)

        def dump(name, ap, shape, dt, reads):
            if not debug:
                return
            t = dram("dbg_" + name, shape, dt, out=True)
            dbg[name] = t
            P.add("sp", I("dma_start", out=t, in_=ap), reads=reads, dma_key="dbg_" + name, is_out=True)

        try:
            P.add("sp", I("dma_start", out=ident_f, in_=ident_d), writes=["ident_f"], dma_key="c0")
            P.add("pool", I("dma_start", out=ident_b, in_=ident_d), writes=["ident_b"], dma_key="c1")
            P.add("sp", I("dma_start", out=hbias, in_=hbias_d), writes=["hbias"], dma_key="c2")
            P.add("dve", I("memset", eps_c, 1e-6), writes=["eps"])

            o = PB
            xs = [V(o + i * 4 * KB, 4 * KB) for i in range(12)]; o += 48 * KB
            t_order = [4, 5, 0, 1, 2, 3, 6, 7]
            xs_base = {t_: (i_ % 3) * 4 for i_, t_ in enumerate(t_order)}
            xn = [V(o + i * 2 * KB, 2 * KB, BF16) for i in range(2)]; o += 8 * KB
            junk = V(o, 2 * KB, BF16); o += 2 * KB
            uTh = [V(o + i * 8 * KB, 8 * KB, BF16, "p (k t) -> p k t", k=8) for i in range(2)]; o += 16 * KB
            gmix_b = V(o, 4 * KB); o += 4 * KB
            TOP = 162 * KB
            KT = V(TOP, 16 * KB, BF16, "p (c t) -> p c t", c=2)
            VT = V(TOP + 16 * KB, 16 * KB, BF16, "p (c t) -> p c t", c=2)
            sK2 = V(TOP + 32 * KB, 2 * KB, BF16, "p (c t) -> p c t", c=2)
            sV2 = V(TOP + 34 * KB, 2 * KB, BF16, "p (c t) -> p c t", c=2)
            sK1 = V(TOP + 36 * KB, 512, BF16, "p (c t) -> p c t", c=2)
            sV1 = V(TOP + 36 * KB + 512, 512, BF16, "p (c t) -> p c t", c=2)
            aT = V(199 * KB, 8 * KB, BF16, "p (c t) -> p c t", c=2)

            wsA2 = V(WS_OFF, 24 * KB, BF16, "p (s k n) -> p s k n", s=2, k=8)
            for s_i, col in enumerate([768, 1536]):
                wdma(wsA2[:, s_i], w_in_v[:, :, col:col + 768],
                     [("ws", 0), ("ws", 1), ("ws", 0, "kv"), ("ws", 1, "kv")], "ws")
            wsB3 = V(G3_OFF, 12 * KB, BF16, "p (s k n) -> p s k n", s=3, k=8)

            def wsA_cols(s_i, kc, c):
                g_off = {0: 512, 1: 512, 2: 256, 3: 256, 4: 0, 5: 0}[s_i]
                return wsA2[:, s_i % 2, kc, g_off + c * 128:g_off + (c + 1) * 128]

            def norm_tile(t, src_rows_fn, xs_l, xn_l, ss_t, rs_t, rstd_t, g_b, dst_fn, load=True, tag="x"):
                for s_ in range(4):
                    gs = 4 * t + s_
                    sl = xs_base[gs // 4] + gs % 4
                    if load:
                        P.add("sp", I("dma_start", out=xs_l[sl], in_=src_rows_fn(gs)),
                              writes=[(tag + "s", sl)], dma_key=(tag + "s", sl))
                    P.add("act", I("activation", out=junk, in_=xs_l[sl], func=AF.Square,
                                                                      accum_out=ss_t[:, gs:gs + 1]),
                          reads=[(tag + "s", sl)], writes=["junk", (tag + "ss", gs)])
                P.add("act", I("activation", out=rs_t[:, 4 * t:4 * t + 4], in_=ss_t[:, 4 * t:4 * t + 4],
                                                    func=AF.Sqrt, scale=1.0 / 1024.0, bias=eps_c),
                      reads=[(tag + "ss", 4 * t + i) for i in range(4)] + ["eps"], writes=[(tag + "rs", t)])

            def xn_and_transpose(t, xs_l, xn_l, rstd_t, g_b, gkey, dst_fn, dst_keys_fn, tag="x", rs_t=None, filler=None):
                rs_t = rs if rs_t is None else rs_t
                P.add("dve", I("reciprocal", out=rstd_t[:, 4 * t:4 * t + 4], in_=rs_t[:, 4 * t:4 * t + 4]),
                      reads=[(tag + "rs", t)], writes=[(tag + "rstd", t)])

                def stt(s_):
                    gs = 4 * t + s_
                    sl = xs_base[gs // 4] + gs % 4
                    s2 = gs % 2
                    P.add("dve", I("scalar_tensor_tensor",
                        out=xn_l[s2], in0=xs_l[sl], scalar=rstd_t[:, gs:gs + 1], in1=g_b, op0=ALU.mult, op1=ALU.mult),
                        reads=[(tag + "s", sl), (tag + "rstd", t), gkey], writes=[(tag + "n", s2)])

                stt(0)
                for s_ in range(4):
                    gs = 4 * t + s_
                    s2 = gs % 2
                    if filler is not None:
                        filler(s_)
                    b = bank("all")
                    pbv = psf[b][:, :].bitcast(BF16)
                    for kc in range(8):
                        P.add("pe", I("transpose",
                            pbv[:, kc * 128:(kc + 1) * 128], xn_l[s2][:, kc * 128:(kc + 1) * 128], ident_b),
                            reads=[(tag + "n", s2), "ident_b"], writes=[("ps", b)])
                    if s_ + 1 < 4:
                        stt(s_ + 1)
                    copy_op("act" if s_ % 2 == 0 else "dve", dst_fn(gs),
                            pbv[:, 0:1024].rearrange("p (c t) -> p c t", c=8), [("ps", b)], dst_keys_fn(gs))

            def phaseA_norm(t):
                norm_tile(t, lambda gs: xh[gs * 128:(gs + 1) * 128, :], xs, xn, ss, rs, rstd, gmix_b, None)

            kv_pending = []

            def kv_filler(s_):
                n_ = (len(kv_pending) + (3 - s_)) // (4 - s_)
                for _ in range(n_):
                    kv_pending.pop(0)()

            def phaseA_rest(t):
                if t < 4:
                    hs = t % 2
                    xn_and_transpose(t, xs, xn, rstd, gmix_b, "gmix_b",
                                     lambda gs, hs=hs: uTh[hs][:, 0:8, (gs % 4) * 128:(gs % 4 + 1) * 128],
                                     lambda gs, hs=hs: [("uTh", hs)], filler=kv_filler)
                    jobs = [(0, KT, 512 * t, 512, "KT"), (1, VT, 512 * t, 512, "VT")]
                    if t == 3:
                        jobs += [(2, sK2, 0, 512, "sK2"), (3, sV2, 0, 512, "sV2"),
                                 (4, sK1, 0, 128, "sK1"), (5, sV1, 0, 128, "sV1")]
                    for (s_i, dst, off, n, key) in jobs:
                        for c in range(2):
                            def grp(s_i=s_i, dst=dst, off=off, n=n, key=key, c=c, hs=hs):
                                rhs0 = 512 - n
                                b = dense(lambda ps: ps[:, 0:n],
                                          [(wsA_cols(s_i, kc, c), uTh[hs][:, kc, rhs0:512]) for kc in range(8)],
                                          [("uTh", hs), ("ws", 0), ("ws", 1)])
                                copy_op(evac_eng(), dst[:, c, off:off + n], psf[b][:, 0:n], [("ps", b)], [(key, off // 512)])
                            kv_pending.append(grp)
                    if t == 3:
                        copy_op("act", uThl, uTh[hs][:, :, 384:512], [("uTh", hs)], ["uThl"])
                else:
                    T = t - 4
                    xn_and_transpose(t, xs, xn, rstd, gmix_b, "gmix_b",
                                     lambda gs: uT[:, 0:8, (gs - 16) * 128:(gs - 15) * 128],
                                     lambda gs, T=T: [("uT", T)], filler=kv_filler)

            phaseA_norm(t_order[0])
            P.add("sp", I("dma_start", out=gmix_b, in_=gmix_d), writes=["gmix_b"], dma_key="c3")
            for i_, t in enumerate(t_order):
                if i_ + 1 < 8:
                    phaseA_norm(t_order[i_ + 1])
                phaseA_rest(t)
                if t == 0:
                    P.add("pool", I("dma_start", out=wsB3[:, 1], in_=w_in_v[:, :, 768 + 512:768 + 768]),
                          reads=[("uTh", 0)], writes=[("wsB3", "k")], dma_key="wsB3k")
                    P.add("pool", I("dma_start", out=wsB3[:, 2], in_=w_in_v[:, :, 1536 + 512:1536 + 768]),
                          reads=[("uTh", 0)], writes=[("wsB3", "v")], dma_key="wsB3v")
                if t == 6:
                    assert not kv_pending
                    wl_prefetch()
            while kv_pending:
                kv_pending.pop(0)()
            dump("uT", uT, [128, 8, 2048], BF16, [("uT", i) for i in range(4)])
            wl_prefetch()
            P.barrier()
            if upto == 1:
                raise _Stop()
            wl["phase"] = 1

            pools["all"] = [5, 6, 7]
            o = PB
            btab = V(o, 4 * KB, F32, "p (h q) -> p h q", h=4); o += 4 * KB
            Vp = V(o, 32 * KB, BF16, "p (s h n) -> p s h n", s=32, h=4); o += 32 * KB
            QT2 = V(o, 16 * KB, BF16, "p (c h t) -> p c h t", c=2, h=2); o += 16 * KB
            acc = V(o, 32 * KB, F32, "p (h t) -> p h t", h=4); o += 32 * KB
            NST = 2
            sT = [V(o + i * 2 * KB, 2 * KB) for i in range(NST)]; o += NST * 2 * KB
            NPT = 6
            pTl = [V(o + i * KB, KB, BF16) for i in range(NPT)]; o += NPT * KB
            lnd = V(PB + 36 * KB, 8 * KB)
            for q_ in range(2):
                P.add("pool", I("memset", V(PB + 36 * KB + q_ * 8 * KB, 8 * KB), 0.0),
                      writes=["QT2z"] + [("QT", i) for i in range(4)])
            assert o <= TOP
            Vp_flat = V(PB + 4 * KB, 32 * KB, BF16, "p (s n) -> p s n", n=128)
            import os
            KSKIP = os.environ.get("KSKIP", "").split(",")
            if upto == 20:
                raise _Stop()
            if "memset" not in KSKIP:
                ONE2 = float(np.frombuffer(np.uint32(0x3F803F80).tobytes(), dtype=np.float32)[0])
                for q_ in range(4):
                    P.add("pool", I("memset", V(PB + 4 * KB + q_ * 8 * KB, 8 * KB), ONE2),
                          writes=["Vp_ones"] + [("Vp", i) for i in range(32)])
            KVALL = [("KT", i) for i in range(8)]
            VTALL = [("VT", i) for i in range(8)]
            ws_slot_ctr = [0]

            def ws_slot():
                s = ws_slot_ctr[0] % 2
                ws_slot_ctr[0] += 1
                return s

            sctr = [0]
            pctr = [0]
            first_group = True
            for gi in (2, 1, 0):
                d = (1, 4, 16)[gi]
                nb = 16 // d
                hl = 128 * d
                sl, wsB = wl_get(("B", gi))
                P.add("sp", I("dma_start", out=btab, in_=btab_d[:, gi * 1024:(gi + 1) * 1024].rearrange(
                    "p (h q) -> p h q", h=4)), writes=["btab"], dma_key="btab")
                if gi == 1:
                    copy_op("act", KT[:, :, 2048 - 512:2048], sK2, [("sK2", 0)] + KVALL, [("KT", 3)])
                    copy_op("dve", VT[:, :, 2048 - 512:2048], sV2, [("sV2", 0)] + VTALL, [("VT", 3)])
                if gi == 0:
                    copy_op("act", KT[:, :, 2048 - 128:2048], sK1, [("sK1", 0)] + KVALL, [("KT", 3)])
                    copy_op("dve", VT[:, :, 2048 - 128:2048], sV1, [("sV1", 0)] + VTALL, [("VT", 3)])
                def qkv_group(s_i, c, T):
                    if gi == 2 and s_i > 0:
                        wkeys = [("wsB3", "k" if s_i == 1 else "v"), ("ws", sl, "kv")]
                    else:
                        wkeys = [("ws", sl)]
                    b = dense(lambda ps: ps[:, 0:512],
                              [(wsB[:, s_i, kc, c * 128:(c + 1) * 128], uT[:, kc, T * 512:(T + 1) * 512]) for kc in range(8)],
                              [("uT", T)] + wkeys)
                    if s_i == 0:
                        ev_q = evac_eng()
                        for hh in range(2):
                            copy_op(ev_q, QT2[hh * 64:(hh + 1) * 64, c, hh, T * 512:(T + 1) * 512],
                                    psf[b][hh * 64:(hh + 1) * 64, 0:512], [("ps", b), "QT2z"], [("QT", T)], scale=0.125)
                    elif s_i == 1:
                        copy_op(evac_eng(), KT[:, c, 2048 + T * 512:2048 + (T + 1) * 512], psf[b][:, 0:512],
                                [("ps", b)], [("KT", 4 + T)])
                    else:
                        copy_op(evac_eng(), VT[:, c, 2048 + T * 512:2048 + (T + 1) * 512], psf[b][:, 0:512],
                                [("ps", b)], [("VT", 4 + T)])

                def vp_bank(r, c, js):
                    t0 = 2048 - hl + r
                    b = bank("all")
                    ev_e = evac_eng()
                    for qi, j in enumerate(js):
                        tj = t0 + j * 128 * d
                        mm(psf[b][:, qi * 128:(qi + 1) * 128], VT[:, c, tj:tj + 127 * d + 1:d], ident_b,
                           True, True, VTALL + ["ident_b"], [("ps", b)])
                    st0 = r * (nb + 1) + js[0]
                    ns_ = len(js)
                    copy_op(ev_e, Vp[:, st0:st0 + ns_, 2 * c:2 * c + 2, 0:64],
                            psf[b][:, 0:ns_ * 128].rearrange("p (s h n) -> p s h n", s=ns_, h=2),
                            [("ps", b)], [("Vp", st0 + i_) for i_ in range(ns_)])

                pools["all"] = [5, 6, 7, 0, 1, 2]
                for c in range(2):
                    for T in range(4):
                        qkv_group(2, c, T)
                vp_jobs = []
                for r in range(d):
                    for c in range(2):
                        for j0 in range(0, nb + 1, 4):
                            vp_jobs.append((r, c, list(range(j0, min(j0 + 4, nb + 1)))))
                qk_jobs = [(s_i, c, T) for s_i in (0, 1) for c in range(2) for T in range(4)]
                for i_, (s_i, c, T) in enumerate(qk_jobs):
                    qkv_group(s_i, c, T)
                    n_ = (len(vp_jobs) + (len(qk_jobs) - 1 - i_)) // (len(qk_jobs) - i_)
                    for _ in range(n_):
                        vp_bank(*vp_jobs.pop(0))
                assert not vp_jobs
                pools["all"] = [5, 6, 7]
                if upto == 22 + 10 * (2 - gi):
                    raise _Stop()
                QALL = [("QT", i) for i in range(4)] + ["QT2z"]
                ACCALL = [("acc", i) for i in range(4)]
                events = []
                for r in range(d):
                    for c in range(2):
                        for j in range(nb + 1):
                            qlo, qhi = max(j, 1), min(j + 1, nb)
                            events.append(("S", r, c, j, qlo, qhi))
                            if j >= 1:
                                events.append(("PV", r, 2 * c, j))
                                events.append(("PV", r, 2 * c + 1, j))
                CH = 2
                s_evs = [ev for ev in events if ev[0] == "S"]
                s_pos = {(ev[1], ev[2], ev[3]): i for i, ev in enumerate(s_evs)}
                pv_by_chunk = {}
                for ev in events:
                    if ev[0] == "PV":
                        pv_by_chunk.setdefault(s_pos[(ev[1], ev[2] // 2, ev[3])] // CH, []).append(ev)
                outl = []
                nchunks = (len(s_evs) + CH - 1) // CH
                for k_ in range(nchunks + 1):
                    if k_ < nchunks:
                        outl += s_evs[k_ * CH:(k_ + 1) * CH]
                    if k_ >= 1:
                        outl += pv_by_chunk.get(k_ - 1, [])
                ptile = {}
                pv_states = {}

                def flush_pv(key):
                    st_ = pv_states.get(key)
                    if st_ is None or st_["bank"] is None:
                        return
                    b = st_["bank"]
                    jobs = st_["jobs"]
                    if gi == 2:
                        r = jobs[0][0]
                        dst = acc[:, 0:4, r:r + 127 * 16 + 1:16]
                        src = psf[b][:, :].rearrange("p (h t) -> p h t", h=4)
                        P.add("act", I("activation", out=dst, in_=src, func=AF.Copy),
                              reads=[("ps", b)], writes=ACCALL)
                    else:
                        (r, h, m0) = jobs[0]
                        if gi == 1:
                            dst = acc[:, h, r:r + 511 * 4 + 1:4]
                        else:
                            dst = acc[:, h, (m0 - 1) * 128:(m0 - 1) * 128 + 512]
                        src = psf[b][:, 0:512]
                        P.add("dve", I("tensor_tensor", out=dst, in0=src, in1=dst, op=ALU.add),
                              reads=[("ps", b)] + ACCALL, writes=ACCALL)
                    st_["bank"] = None
                    st_["jobs"] = []

                for ev in outl:
                    if ev[0] == "S":
                        _, r, c, j, qlo, qhi = ev
                        n = 128 * (qhi - qlo + 1)
                        col0 = 0 if qlo == j else 128
                        kt0 = 2048 - hl + j * hl + r
                        q0 = (qlo - 1) * hl + r
                        b = bank("s")
                        si = sctr[0] % NST
                        sctr[0] += 1
                        pi = pctr[0] % NPT
                        pctr[0] += 1
                        ptile[(r, c, j)] = (pi, qlo, n)
                        mm(psf[b][:, 0:2 * n], KT[:, c, kt0:kt0 + 127 * d + 1:d],
                           QT2[:, c, :, q0:q0 + (n - 1) * d + 1:d], True, True, KVALL + QALL, [("ps", b)])
                        P.add("dve", I("tensor_tensor",
                            out=sT[si][:, 0:2 * n].rearrange("p (h q) -> p h q", h=2),
                            in0=psf[b][:, 0:2 * n].rearrange("p (h q) -> p h q", h=2),
                            in1=btab[:, 2 * c:2 * c + 2, col0:col0 + n], op=ALU.add),
                            reads=[("ps", b), "btab"], writes=[("sT", si)])
                        if j == 0:
                            P.add("act", I("activation",
                                out=pTl[pi][:, 0:2 * n], in_=sT[si][:, 0:2 * n], func=AF.Exp, bias=hbias),
                                reads=[("sT", si), "hbias"], writes=[("pT", pi)])
                        else:
                            P.add("act", I("activation",
                                out=pTl[pi][:, 0:2 * n], in_=sT[si][:, 0:2 * n], func=AF.Exp),
                                reads=[("sT", si)], writes=[("pT", pi)])
                    else:
                        _, r, h, m = ev
                        c, hh = h // 2, h % 2
                        key = 0 if gi == 2 else hh
                        st_ = pv_states.setdefault(key, {"bank": None, "jobs": []})
                        if st_["bank"] is None:
                            st_["bank"] = bank("pv")
                        b = st_["bank"]
                        q4 = len(st_["jobs"])
                        st_["jobs"].append((r, h, m))
                        st_prev = r * (nb + 1) + (m - 1)
                        st_cur = r * (nb + 1) + m
                        pi0, qlo0, n0 = ptile[(r, c, m - 1)]
                        pi1, qlo1, n1 = ptile[(r, c, m)]
                        o_ap = psf[b][:, q4 * 128:(q4 + 1) * 128]
                        a0 = hh * n0 + (m - qlo0) * 128
                        a1 = hh * n1 + (m - qlo1) * 128
                        mm(o_ap, Vp[:, st_prev, h, :], pTl[pi0][:, a0:a0 + 128], True, False,
                           [("Vp", st_prev), "Vp_ones", ("pT", pi0)], [("ps", b)])
                        mm(o_ap, Vp[:, st_cur, h, :], pTl[pi1][:, a1:a1 + 128], False, True,
                           [("Vp", st_cur), "Vp_ones", ("pT", pi1)], [("ps", b)])
                        if len(st_["jobs"]) == 4:
                            flush_pv(key)
                for key in list(pv_states):
                    flush_pv(key)
                if upto == 23 + (2 - gi):
                    raise _Stop()
            wl_prefetch()
            P.barrier()
            if upto == 2:
                raise _Stop()
            wl["phase"] = 2
            pools["all"] = [0, 1, 2, 3, 4, 5, 6, 7]
            ACCALL = [("acc", i) for i in range(4)]
            dump("acc", acc, [128, 4, 2048], F32, ACCALL)

            pT = V(PB, 24 * KB, BF16, "p (k t) -> p k t", k=6)
            pooled = V(PB + 32 * KB, 32 * KB, BF16, "p (k t) -> p k t", k=8)
            o = 180 * KB
            ptab = V(o, 2 * KB, BF16, "p (g r t) -> p g r t", g=4, r=2); o += 2 * KB
            ptab0 = V(o, 1 * KB, BF16, "p (g t) -> p g t", g=4); o += 1 * KB
            pscale = V(o, 32); o += 32
            wpad = V(o, 2 * KB, BF16, "p (s a k n) -> p s a k n", s=2, a=2, k=2); o += 2 * KB
            wpad_f = V(o - 2 * KB, 2 * KB)
            pzk = V(152 * KB, 17 * 1536, BF16, "p (b n) -> p b n", n=768)
            sl, (wpz, wgr) = wl_get(("C1",))
            P.add("sp", I("dma_start", out=pscale[:, 0:6], in_=pscale_d), writes=["pscale"], dma_key="c4")
            P.add("pool", I("dma_start", out=ptab, in_=ptab_d.rearrange("p (g r t) -> p g r t", g=4, r=2)),
                  writes=["ptab"], dma_key="c5")
            P.add("pool", I("dma_start", out=ptab0, in_=ptab0_d.rearrange("p (g t) -> p g t", g=4)),
                  writes=["ptab0"], dma_key="c8")
            P.add("dve", I("memset", wpad_f, 0.0), writes=["wpad"])
            for s_ in range(2):
                P.add("pool", I("dma_start", out=wpad[0:96, s_, 0, :, 0:64], in_=w_grp_v[:, 2 * s_, :, 128:192]),
                      writes=["wpad"], dma_key="c9")
                P.add("pool", I("dma_start", out=wpad[0:96, s_, 1, :, 64:128], in_=w_grp_v[:, 2 * s_ + 1, :, 0:64]),
                      writes=["wpad"], dma_key="c9")
            for bl in range(17):
                for hf in range(2):
                    lhs = (lambda kc: uThl[:, kc, :]) if bl == 0 else (lambda kc, bl=bl: uT[:, kc, (bl - 1) * 128:bl * 128])
                    rd = ["uThl"] if bl == 0 else [("uT", (bl - 1) // 4)]
                    b = dense(lambda ps: ps[:, 0:384],
                              [(lhs(kc), wpz[:, kc, hf * 384:(hf + 1) * 384]) for kc in range(8)], rd + [("ws", sl)])
                    copy_op(evac_eng(), pzk[:, bl, hf * 384:(hf + 1) * 384], psf[b][:, 0:384], [("ps", b)], [("pzk", bl)])
                if bl in (3, 6, 9, 12):
                    h = bl // 3 - 1
                    c, pb = h // 2, (h % 2) * 64
                    P.add("act", I("activation", out=lnd[64:128, :], in_=acc[64:128, h, :], func=AF.Ln),
                          reads=ACCALL, writes=["lnd_hi"])
                    P.add("act", I("activation", out=lnd[0:64, :], in_=lnd[64:128, :], func=AF.Exp, scale=-1.0),
                          reads=["lnd_hi"], writes=["lnd_lo"])
                    P.add("dve", I("tensor_tensor",
                        out=aT[pb:pb + 64, c, :], in0=acc[0:64, h, :], in1=lnd[0:64, :], op=ALU.mult),
                        reads=ACCALL + ["lnd_lo"], writes=["aT"])
            dump("aT", aT, [128, 2, 2048], BF16, ["aT"])
            P.barrier()
            if upto == 3:
                raise _Stop()
            for g in range(4):
                for dc2 in range(2):
                    ch = 2 * g + dc2
                    for q0 in range(4):
                        b = bank("all")
                        for q in range(4):
                            bl = 1 + q0 * 4 + q
                            o_ap = psf[b][0:96, q * 128:(q + 1) * 128]
                            cur = ptab0[:, g, :] if bl == 1 else ptab[:, g, 1, :]
                            mm(o_ap, pzk[:, bl - 1, ch * 96:(ch + 1) * 96], ptab[:, g, 0, :], True, False,
                               [("pzk", bl - 1), "ptab"], [("ps", b)])
                            mm(o_ap, pzk[:, bl, ch * 96:(ch + 1) * 96], cur, False, True,
                               [("pzk", bl), "ptab", "ptab0"], [("ps", b)])
                        copy_op(evac_eng(), pooled[0:96, ch, q0 * 512:(q0 + 1) * 512], psf[b][0:96, 0:512],
                                [("ps", b)], [("pooled", ch, q0)])
                def pl_keys(gg, T):
                    return [("pooled", 2 * gg, T), ("pooled", 2 * gg + 1, T)]

                chunks = {0: [0], 1: [1, 2], 2: [3], 3: [4, 5]}[g]
                for ch6 in chunks:
                    for T in range(4):
                        ts = slice(T * 512, (T + 1) * 512)
                        if ch6 in (1, 4):
                            s_ = ch6 // 3
                            pairs = [(wpad[0:96, s_, 0, kc, :], pooled[0:96, 2 * (2 * s_) + kc, ts]) for kc in range(2)] + \
                                    [(wpad[0:96, s_, 1, kc, :], pooled[0:96, 2 * (2 * s_ + 1) + kc, ts]) for kc in range(2)]
                            rd = pl_keys(2 * s_, T) + pl_keys(2 * s_ + 1, T) + ["wpad"]
                        else:
                            gg = {0: 0, 2: 1, 3: 2, 5: 3}[ch6]
                            c_lo = 0 if ch6 in (0, 3) else 64
                            pairs = [(wgr[0:96, gg, kc, c_lo:c_lo + 128], pooled[0:96, 2 * gg + kc, ts]) for kc in range(2)]
                            rd = pl_keys(gg, T) + [("ws", sl)]
                        b = dense(lambda ps: ps[:, 0:512], pairs, rd)
                        P.add("act", I("activation",
                            out=pT[:, ch6, ts], in_=psf[b][:, 0:512], func=AF.Copy, scale=pscale[:, ch6:ch6 + 1]),
                            reads=[("ps", b), "pscale"], writes=["pT"])
            dump("pT", pT, [128, 6, 2048], BF16, ["pT"])
            wl_prefetch()
            P.barrier()
            if upto == 4:
                raise _Stop()
            wl["phase"] = 3

            o = PB + 32 * KB
            mergedT = V(o, 32 * KB, BF16, "p (k t) -> p k t", k=8); o += 32 * KB
            sg = [[V(o + (i * 4 + j) * 2 * KB, 2 * KB) for j in range(4)] for i in range(2)]; o += 16 * KB
            it = 0
            for jp in range(4):
                sl, (wga, wgp, wat, wpo) = wl_get(("C2a", jp))
                for je in range(2):
                    j = 2 * jp + je
                    cs = slice(je * 128, (je + 1) * 128)
                    for T in range(4):
                        ts = slice(T * 512, (T + 1) * 512)
                        bga = dense(lambda ps: ps[:, 0:512], [(wga[:, kc, cs], uT[:, kc, ts]) for kc in range(8)],
                                    [("uT", T), ("ws", sl)])
                        bgp = dense(lambda ps: ps[:, 0:512], [(wgp[:, kc, cs], uT[:, kc, ts]) for kc in range(8)],
                                    [("uT", T), ("ws", sl)])
                        ba = dense(lambda ps: ps[:, 0:512], [(wat[:, kc, cs], aT[:, kc, ts]) for kc in range(2)],
                                   ["aT", ("ws", sl)])
                        bp = dense(lambda ps: ps[:, 0:512], [(wpo[:, kc, cs], pT[:, kc, ts]) for kc in range(6)],
                                   ["pT", ("ws", sl)])
                        s_ = sg[it % 2]
                        ik = it % 2
                        it += 1
                        P.add("act", I("activation", out=s_[0], in_=psf[bga][:, 0:512], func=AF.Sigmoid),
                              reads=[("ps", bga)], writes=[("sg", ik, 0)])
                        P.add("act", I("activation", out=s_[1], in_=psf[bgp][:, 0:512], func=AF.Sigmoid),
                              reads=[("ps", bgp)], writes=[("sg", ik, 1)])
                        P.add("dve", I("tensor_tensor", out=s_[2], in0=psf[ba][:, 0:512], in1=s_[0], op=ALU.mult),
                              reads=[("ps", ba), ("sg", ik, 0)], writes=[("sg", ik, 2)])
                        P.add("dve", I("tensor_tensor", out=s_[3], in0=psf[bp][:, 0:512], in1=s_[1], op=ALU.mult),
                              reads=[("ps", bp), ("sg", ik, 1)], writes=[("sg", ik, 3)])
                        P.add("dve", I("tensor_tensor", out=mergedT[:, j, ts], in0=s_[2], in1=s_[3], op=ALU.add),
                              reads=[("sg", ik, 2), ("sg", ik, 3)], writes=[("mergedT", T)])
            dump("mergedT", mergedT, [128, 8, 2048], BF16, [("mergedT", i) for i in range(4)])
            wl_prefetch()
            P.barrier()
            if upto == 5:
                raise _Stop()
            wl["phase"] = 4

            hbuf = V(132 * KB, 64 * KB, F32, "p (s n) -> p s n", s=16)
            o = PB
            mn = [V(o + i * 2 * KB, 2 * KB, BF16) for i in range(2)]; o += 8 * KB
            gmlp_b = V(o, 4 * KB); o += 4 * KB
            junk = V(o, 2 * KB, BF16); o += 2 * KB
            P.add("sp", I("dma_start", out=gmlp_b, in_=gmlp_d), writes=["gmlp_b"], dma_key="c6")
            sl, wo = wl_get(("C2b",))
            for s_ in range(16):
                P.add("sp", I("dma_start", out=hbuf[:, s_, :], in_=xh[2048 + s_ * 128:2048 + (s_ + 1) * 128, :]),
                      writes=[("h", s_)], dma_key=("hld", s_))
            mT = uT
            def c2b_dense_sub(s_):
                T = s_ // 4
                for half in range(2):
                    es_ = slice(half * 512, (half + 1) * 512)
                    b = dense(lambda ps: ps[:, 0:512],
                              [(mergedT[:, kc, s_ * 128:(s_ + 1) * 128], wo[:, kc, es_]) for kc in range(8)],
                              [("mergedT", T), ("ws", sl)])
                    P.add("dve", I("tensor_tensor",
                        out=hbuf[:, s_, es_], in0=psf[b][:, 0:512], in1=hbuf[:, s_, es_], op=ALU.add),
                        reads=[("ps", b), ("h", s_)], writes=[("h", s_)])
                P.add("act", I("activation", out=junk, in_=hbuf[:, s_, :], func=AF.Square,
                               accum_out=ss2[:, s_:s_ + 1]),
                      reads=[("h", s_)], writes=["junk", ("ss2", s_)])
                if s_ % 4 == 3:
                    P.add("act", I("activation", out=rs2[:, 4 * T:4 * T + 4], in_=ss2[:, 4 * T:4 * T + 4],
                                   func=AF.Sqrt, scale=1.0 / 1024.0, bias=eps_c),
                          reads=[("ss2", 4 * T + i) for i in range(4)] + ["eps"], writes=[("rs2", T)])

            def c2b_stt(sq):
                T = sq // 4
                s2 = sq % 2
                if sq % 4 == 0:
                    P.add("dve", I("reciprocal", out=rstd2[:, 4 * T:4 * T + 4], in_=rs2[:, 4 * T:4 * T + 4]),
                          reads=[("rs2", T)], writes=[("rstd2", T)])
                P.add("dve", I("scalar_tensor_tensor",
                    out=mn[s2], in0=hbuf[:, sq, :], scalar=rstd2[:, sq:sq + 1], in1=gmlp_b, op0=ALU.mult, op1=ALU.mult),
                    reads=[("h", sq), ("rstd2", T), "gmlp_b"], writes=[("mn", s2)])

            def c2b_tr(sq):
                T = sq // 4
                s2 = sq % 2
                b = bank("all")
                pbv = psf[b][:, :].bitcast(BF16)
                for kc in range(8):
                    P.add("pe", I("transpose",
                        pbv[:, kc * 128:(kc + 1) * 128], mn[s2][:, kc * 128:(kc + 1) * 128], ident_b),
                        reads=[("mn", s2), "ident_b"], writes=[("ps", b)])
                return b, pbv

            def c2b_evac(sq, b, pbv):
                T = sq // 4
                copy_op("act" if sq % 2 == 0 else "dve", mT[:, 0:8, sq * 128:(sq + 1) * 128],
                        pbv[:, 0:1024].rearrange("p (c t) -> p c t", c=8), [("ps", b)], [("uT", T)])

            for s_ in range(4):
                c2b_dense_sub(s_)
            c2b_stt(0)
            for sq in range(16):
                if sq + 4 < 16:
                    c2b_dense_sub(sq + 4)
                b_, pbv_ = c2b_tr(sq)
                if sq + 1 < 16:
                    c2b_stt(sq + 1)
                c2b_evac(sq, b_, pbv_)
            dump("h", hbuf, [128, 16, 1024], F32, [("h", i) for i in range(16)])
            wl_prefetch()
            P.barrier()
            if upto == 6:
                raise _Stop()
            wl["phase"] = 5

            o = PB
            hidT = [V(o + i * 16 * KB, 16 * KB, BF16, "p (k t) -> p k t", k=4) for i in range(2)]; o += 32 * KB
            rtmp = [V(o + i * 2 * KB, 2 * KB) for i in range(2)]; o += 4 * KB
            junk = V(o, 2 * KB, BF16); o += 2 * KB
            assert o <= S3_OFF
            gfin_b = V(122 * KB, 4 * KB)
            P.add("sp", I("dma_start", out=gfin_b, in_=gfin_d), writes=["gfin_b"], dma_key="c7")
            NFG = 8
            wsl = {}
            wl["auto"] = False

            def load_fg(fg):
                sl, (w1, w2) = wl_get(("D", fg))
                wsl[fg] = (sl, w1, w2)

            rc = [0]

            def mlp_in(fg):
                sl, w1, w2 = wsl[fg]
                hT = hidT[fg % 2]
                for fc in range(4):
                    for T in range(4):
                        ts = slice(T * 512, (T + 1) * 512)
                        b = dense(lambda ps: ps[:, 0:512],
                                  [(w1[:, kc, fc * 128:(fc + 1) * 128], mT[:, kc, ts]) for kc in range(8)],
                                  [("uT", T), ("ws", sl)])
                        ri = rc[0] % 2
                        rc[0] += 1
                        P.add("act", I("activation", out=rtmp[ri], in_=psf[b][:, 0:512], func=AF.Relu),
                              reads=[("ps", b)], writes=[("rtmp", ri)])
                        P.add("dve", I("scalar_tensor_tensor",
                            out=hT[:, fc, ts], in0=psf[b][:, 0:512], scalar=0.0, in1=rtmp[ri], op0=ALU.max, op1=ALU.mult),
                            reads=[("ps", b), ("rtmp", ri)], writes=[("hidT", fg % 2, T)])

            def final_norm(T):
                for s_ in range(4 * T, 4 * T + 4):
                    P.add("act", I("activation", out=junk, in_=hbuf[:, s_, :], func=AF.Square,
                                   accum_out=ss3[:, s_:s_ + 1]),
                          reads=[("h", s_)], writes=["junk", ("ss3", s_)])
                P.add("act", I("activation", out=rs3[:, 4 * T:4 * T + 4], in_=ss3[:, 4 * T:4 * T + 4],
                               func=AF.Sqrt, scale=1.0 / 1024.0, bias=eps_c),
                      reads=[("ss3", 4 * T + i) for i in range(4)] + ["eps"], writes=[("rs3", T)])
                P.add("dve", I("reciprocal", out=rstd3[:, 4 * T:4 * T + 4], in_=rs3[:, 4 * T:4 * T + 4]),
                      reads=[("rs3", T)], writes=[("rstd3", T)])
                for sq in range(4 * T, 4 * T + 4):
                    P.add("dve", I("scalar_tensor_tensor",
                        out=hbuf[:, sq, :], in0=hbuf[:, sq, :], scalar=rstd3[:, sq:sq + 1], in1=gfin_b,
                        op0=ALU.mult, op1=ALU.mult),
                        reads=[("h", sq), ("rstd3", T), "gfin_b"], writes=[("h", sq)])
                    P.add("sp", I("dma_start", out=y[sq * 128:(sq + 1) * 128, :], in_=hbuf[:, sq, :]),
                          reads=[("h", sq)], dma_key=("yst", sq), is_out=True)

            def mlp_out(fg):
                sl, w1, w2 = wsl[fg]
                hT = hidT[fg % 2]
                for s_ in range(16):
                    T = s_ // 4
                    for half in range(2):
                        es_ = slice(half * 512, (half + 1) * 512)
                        b = dense(lambda ps: ps[:, 0:512],
                                  [(hT[:, fc, s_ * 128:(s_ + 1) * 128], w2[:, fc, es_]) for fc in range(4)],
                                  [("hidT", fg % 2, T), ("ws", sl)])
                        P.add("dve", I("tensor_tensor",
                            out=hbuf[:, s_, es_], in0=psf[b][:, 0:512], in1=hbuf[:, s_, es_], op=ALU.add),
                            reads=[("ps", b), ("h", s_)], writes=[("h", s_)])
                    if fg == NFG - 1 and s_ % 4 == 3 and T >= 1:
                        final_norm(T - 1)
                if fg == NFG - 1:
                    final_norm(3)

            load_fg(0)
            load_fg(1)
            load_fg(2)
            mlp_in(0)
            for fg in range(NFG):
                if fg + 1 < NFG:
                    mlp_in(fg + 1)
                mlp_out(fg)
                if fg + 3 < NFG:
                    load_fg(fg + 3)

        except _Stop:
            pass
        names = ["eng_" + e for e in Prog.ENGS] + ["dma_%d" % i for i in range(len(P.dma_keys))]
        sems = {n: es.enter_context(nc.semaphore(n)) for n in names}
        block = es.enter_context(nc.Block())
        P.emit(nc, block, sems)
    return nc, dbg


def host_inputs(x, norm_mix_g, w_in, w_att_out, w_pool_grp, pool_scale, w_pool_out, w_out,
                norm_mlp_g, w_mlp_in, w_mlp_out, norm_final_g):
    f = np.float32
    x2 = np.asarray(x, f)[0]
    common = {
        "w_in": np.ascontiguousarray(np.asarray(w_in, f)[0]),
        "w_att": np.ascontiguousarray(np.asarray(w_att_out, f)[0]),
        "w_grp": np.ascontiguousarray(np.asarray(w_pool_grp, f)[0].reshape(768, 192)),
        "w_po": np.ascontiguousarray(np.asarray(w_pool_out, f)[0]),
        "w_out": np.ascontiguousarray(np.asarray(w_out, f)[0]),
        "w_m1": np.ascontiguousarray(np.asarray(w_mlp_in, f)[0]),
        "w_m2": np.ascontiguousarray(np.asarray(w_mlp_out, f)[0]),
        "gmix_b": np.ascontiguousarray(np.broadcast_to(np.asarray(norm_mix_g, f)[0], (128, 1024))),
        "gmlp_b": np.ascontiguousarray(np.broadcast_to(np.asarray(norm_mlp_g, f)[0], (128, 1024))),
        "gfin_b": np.ascontiguousarray(np.broadcast_to(np.asarray(norm_final_g, f), (128, 1024))),
        "pscale": np.ascontiguousarray(np.asarray(pool_scale, f)[0].reshape(6, 128).T),
        "ident": np.eye(128, dtype=f),
    }
    slopes = 2.0 ** (-8.0 * (np.arange(12, dtype=np.float64) + 1.0) / 12.0)
    k_i = np.arange(128)[:, None]
    q_i = np.arange(256)[None, :]
    steps = q_i - k_i
    valid = (steps >= 0) & (steps <= 128)
    bt = np.zeros((128, 12, 256), f)
    for hd in range(12):
        dil = (1, 4, 16)[hd // 4]
        bt[:, hd, :] = np.where(valid, -(slopes[hd] * steps * dil), NEG).astype(f)
    common["btab"] = np.ascontiguousarray(bt.reshape(128, 12 * 256))
    tk = np.arange(128)[:, None]
    tq = np.arange(128)[None, :]
    pt = np.zeros((128, 4, 2, 128), f)
    for g, w in enumerate((2, 4, 8, 16)):
        dlt = tq - tk
        pt[:, g, 1, :] = np.where((dlt >= 0) & (dlt <= w - 1), 1.0 / w, 0.0) - (dlt == 0)
        dlp = tq + 128 - tk
        pt[:, g, 0, :] = np.where(dlp <= w - 1, 1.0 / w, 0.0)
    common["ptab"] = np.ascontiguousarray(pt.reshape(128, 1024))
    in_maps = []
    for c in range(NCORES):
        xhc = np.zeros((4096, 1024), f)
        if c > 0:
            xhc[0:2048] = x2[(c - 1) * TOK:c * TOK]
        xhc[2048:] = x2[c * TOK:(c + 1) * TOK]
        hb = np.full((128, 1), NEG if c == 0 else 0.0, f)
        p0 = np.zeros((128, 4, 128), f)
        for g, w in enumerate((2, 4, 8, 16)):
            dlt = tq - tk
            cnt = np.minimum(tq + 1, w) if c == 0 else w
            p0[:, g, :] = np.where((dlt >= 0) & (dlt <= w - 1), 1.0 / cnt, 0.0) - (dlt == 0)
        m = dict(common)
        m["xh"] = xhc
        m["hbias"] = hb
        m["ptab0"] = np.ascontiguousarray(p0.reshape(128, 512))
        in_maps.append(m)
    return in_maps


_CACHE = {}


def kernel(**inputs):
    in_maps = host_inputs(**inputs)
    if "nc" not in _CACHE:
        _CACHE["nc"] = build(False)[0]
    nc = _CACHE["nc"]
    res = run_bass_kernel_spmd(nc, in_maps, core_ids=list(range(NCORES)))
    out = np.concatenate([np.asarray(r["y"], np.float32) for r in res.results], axis=0)
    return out.reshape(1, NCORES * TOK, 1024)
```

```python
from contextlib import ExitStack
import numpy as np
import concourse.bass as bass
import concourse.mybir as mybir
from concourse.bass_utils import run_bass_kernel_spmd

F32 = mybir.dt.float32
BF16 = mybir.dt.bfloat16
ALU = mybir.AluOpType
AF = mybir.ActivationFunctionType

NCORES = 8
TOK = 2048
KB = 1024
ARENA_KB = 207
NEG = -30000.0


def I(name, *a, **k):
    return lambda e: getattr(e, name)(*a, **k)


class Op:
    __slots__ = ("eng", "fn", "deps", "dma_key", "seq", "has_dep", "idx", "is_out")


class Prog:
    ENGS = ("pe", "act", "dve", "pool", "sp")

    def __init__(self):
        self.ops = []
        self.last_w = {}
        self.readers = {}
        self.dma_keys = []
        self.last_eng = {}
        self.last_dma = {}
        self.bar_deps = set()

    def add(self, eng, fn, reads=(), writes=(), dma_key=None, is_out=False):
        op = Op()
        op.eng = eng
        op.fn = fn
        op.idx = len(self.ops)
        op.dma_key = dma_key
        op.has_dep = False
        op.seq = 0
        op.is_out = is_out
        deps = set(self.bar_deps)
        for k in reads:
            w = self.last_w.get(k)
            if w is not None:
                deps.add(w)
        for k in writes:
            w = self.last_w.get(k)
            if w is not None:
                deps.add(w)
            for r in self.readers.get(k, ()):
                deps.add(r)
        op.deps = deps
        for k in reads:
            self.readers.setdefault(k, []).append(op.idx)
        for k in writes:
            self.last_w[k] = op.idx
            self.readers[k] = []
        if dma_key is not None:
            if dma_key not in self.dma_keys:
                self.dma_keys.append(dma_key)
            self.last_dma[dma_key] = op.idx
        else:
            self.last_eng[eng] = op.idx
        self.ops.append(op)
        return op

    def barrier(self):
        self.bar_deps = set(self.last_eng.values()) | set(self.last_dma.values())

    def emit(self, nc, block, sems):
        ops = self.ops
        for op in ops:
            for d in op.deps:
                dop = ops[d]
                if dop.dma_key is None and not (dop.eng == "pe" and op.eng == "pe"):
                    dop.has_dep = True
        cnt = {e: 0 for e in self.ENGS}
        for op in ops:
            if op.dma_key is None and op.has_dep:
                cnt[op.eng] += 1
                op.seq = cnt[op.eng]
        dma_cum = {}
        dma_at = {}
        run = {}
        for op in ops:
            if op.dma_key is not None:
                run[op.dma_key] = run.get(op.dma_key, 0) + 16
                dma_cum.setdefault(op.dma_key, []).append((op.idx, run[op.dma_key]))
                dma_at[op.idx] = run[op.dma_key]
        eng_sem = {e: sems["eng_" + e] for e in self.ENGS}
        dma_sem = {k: sems["dma_%d" % i] for i, k in enumerate(self.dma_keys)}

        def cum_before(key, idx):
            v = 0
            for (i, c) in dma_cum[key]:
                if i < idx:
                    v = c
                else:
                    break
            return v

        def run_engine(eng, e):
            waited = {}
            for op in ops:
                if op.eng != eng:
                    continue
                for d in sorted(op.deps):
                    dop = ops[d]
                    if dop.dma_key is not None:
                        s = dma_sem[dop.dma_key]
                        val = dma_at[dop.idx]
                        key = ("d", dop.dma_key)
                    else:
                        if dop.eng == "pe" and eng == "pe":
                            continue
                        s = eng_sem[dop.eng]
                        val = dop.seq
                        key = ("e", dop.eng)
                    if waited.get(key, 0) >= val:
                        continue
                    e.wait_ge(s, val)
                    waited[key] = val
                inst = op.fn(e)
                if op.dma_key is not None:
                    inst.then_inc(dma_sem[op.dma_key], 16)
                elif op.has_dep:
                    inst.then_inc(eng_sem[eng], 1)
            if eng == "sp":
                for k in self.dma_keys:
                    if any(o.is_out for o in ops if o.dma_key == k):
                        e.wait_ge(dma_sem[k], dma_cum[k][-1][1])

        @block.tensor
        def _(e):
            run_engine("pe", e)

        @block.scalar
        def _(e):
            run_engine("act", e)

        @block.vector
        def _(e):
            run_engine("dve", e)

        @block.gpsimd
        def _(e):
            run_engine("pool", e)

        @block.sync
        def _(e):
            run_engine("sp", e)


class _Stop(Exception):
    pass


def build(debug=False, upto=None):
    nc = bass.Bass("TRN2", target_bir_lowering=False)

    def dram(name, shape, dt=F32, out=False):
        return nc.dram_tensor(name, shape, dt, kind="ExternalOutput" if out else "ExternalInput").ap()

    xh = dram("xh", [4096, 1024])
    w_in = dram("w_in", [1024, 5120])
    w_att = dram("w_att", [256, 1024])
    w_grp = dram("w_grp", [768, 192])
    w_po = dram("w_po", [768, 1024])
    w_out = dram("w_out", [1024, 1024])
    w_m1 = dram("w_m1", [1024, 4096])
    w_m2 = dram("w_m2", [4096, 1024])
    gmix_d = dram("gmix_b", [128, 1024])
    gmlp_d = dram("gmlp_b", [128, 1024])
    gfin_d = dram("gfin_b", [128, 1024])
    pscale_d = dram("pscale", [128, 6])
    btab_d = dram("btab", [128, 12 * 256])
    hbias_d = dram("hbias", [128, 1])
    ptab_d = dram("ptab", [128, 4 * 2 * 128])
    ptab0_d = dram("ptab0", [128, 4 * 128])
    ident_d = dram("ident", [128, 128])
    y = dram("y", [TOK, 1024], out=True)
    dbg = {}

    P = Prog()
    es = ExitStack()
    with es:
        arena = es.enter_context(nc.sbuf_tensor("arena", [128, ARENA_KB * 256], F32))
        psf = [es.enter_context(nc.psum_tensor("psf%d" % i, [128, 512], F32)) for i in range(8)]

        def V(off, nbytes, dt=F32, pat=None, **kw):
            assert off % 4 == 0 and nbytes % 4 == 0 and off + nbytes <= ARENA_KB * KB, (off, nbytes)
            a = arena[:, off // 4:(off + nbytes) // 4]
            if dt == BF16:
                a = a.bitcast(BF16)
            if pat:
                a = a.rearrange(pat, **kw)
            return a

        uT = V(0, 32 * KB, BF16, "p (k t) -> p k t", k=8)
        WS_OFF = 32 * KB
        ident_f = V(64 * KB, 512)
        ident_b = V(64 * KB + 512, 256, BF16)
        small = V(64 * KB + 768, 1024)
        eps_c = small[:, 0:1]
        hbias = small[:, 1:2]
        ss = small[:, 32:64]
        rs = small[:, 64:96]
        rstd = small[:, 96:128]
        ss2 = small[:, 128:144]
        rs2 = small[:, 144:160]
        rstd2 = small[:, 160:176]
        ss3 = small[:, 176:192]
        rs3 = small[:, 192:208]
        rstd3 = small[:, 208:224]
        uThl = V(66 * KB, 2 * KB, BF16, "p (k t) -> p k t", k=8)
        PB = 68 * KB

        w_in_v = w_in.rearrange("(k p) n -> p k n", p=128)
        w_att_v = w_att.rearrange("(k p) n -> p k n", p=128)
        w_po_v = w_po.rearrange("(k p) n -> p k n", p=128)
        w_grp_v = w_grp.rearrange("(g k p) n -> p g k n", g=4, k=2, p=96)
        w_out_v = w_out.rearrange("(k p) n -> p k n", p=128)
        w_m1_v = w_m1.rearrange("(k p) n -> p k n", p=128)
        w_m2_v = w_m2.rearrange("(k p) n -> p k n", p=128)
        S3_OFF = 106 * KB

        wl = {"sched": [], "next": 0, "res": {}, "ctr": 0, "phase": 0, "last": -1, "auto": True}

        def wl_plan(key, phase, fn, early=True, slot=None):
            wl["sched"].append((key, phase, fn, early, slot))

        def wl_issue():
            key, phase, fn, early, slot = wl["sched"][wl["next"]]
            wl["next"] += 1
            if slot is None:
                sl = wl["ctr"] % 2
                wl["ctr"] += 1
            else:
                sl = slot()
            base = S3_OFF if sl == 2 else WS_OFF + sl * 16 * KB
            wl["res"][key] = (sl, fn(base, sl))

        def wl_prefetch():
            if wl["next"] >= len(wl["sched"]) or wl["next"] > wl["last"] + 1:
                return
            key, phase, fn, early, slot = wl["sched"][wl["next"]]
            if phase == wl["phase"] or (phase == wl["phase"] + 1 and early):
                wl_issue()

        def wl_get(key):
            i = [k_[0] for k_ in wl["sched"]].index(key)
            while wl["next"] <= i:
                wl_issue()
            wl["last"] = i
            if wl["auto"]:
                wl_prefetch()
            return wl["res"][key]

        def wd(out, in_, sl):
            P.add("pool", I("dma_start", out=out, in_=in_), writes=[("ws", sl), ("ws", sl, "kv")],
                  dma_key=("wsd", sl))

        G3_OFF = WS_OFF + 20 * KB

        def ld_B(gi):
            def fn(base, sl):
                if gi == 2:
                    wsB = V(G3_OFF, 12 * KB, BF16, "p (s k n) -> p s k n", s=3, k=8)
                    wd(wsB[:, 0], w_in_v[:, :, 256 * gi:256 * gi + 256], sl)
                    return wsB
                wsB = V(base, 12 * KB, BF16, "p (s k n) -> p s k n", s=3, k=8)
                for s_i, col in enumerate([256 * gi, 768 + 256 * gi, 1536 + 256 * gi]):
                    wd(wsB[:, s_i], w_in_v[:, :, col:col + 256], sl)
                return wsB
            return fn

        def ld_C1(base, sl):
            wpz = V(base, 12 * KB, BF16, "p (k n) -> p k n", k=8)
            wgr = V(base + 12 * KB, 3 * KB, BF16, "p (g k n) -> p g k n", g=4, k=2)
            wd(wpz, w_in_v[:, :, 2304:3072], sl)
            wd(wgr[0:96], w_grp_v, sl)
            return (wpz, wgr)

        def ld_C2a(jp):
            def fn(base, sl):
                wga = V(base, 4 * KB, BF16, "p (k n) -> p k n", k=8)
                wgp = V(base + 4 * KB, 4 * KB, BF16, "p (k n) -> p k n", k=8)
                wat = V(base + 8 * KB, 1 * KB, BF16, "p (k n) -> p k n", k=2)
                wpo = V(base + 9 * KB, 3 * KB, BF16, "p (k n) -> p k n", k=6)
                c0 = 256 * jp
                wd(wga, w_in_v[:, :, 3072 + c0:3072 + c0 + 256], sl)
                wd(wgp, w_in_v[:, :, 4096 + c0:4096 + c0 + 256], sl)
                wd(wat, w_att_v[:, :, c0:c0 + 256], sl)
                wd(wpo, w_po_v[:, :, c0:c0 + 256], sl)
                return (wga, wgp, wat, wpo)
            return fn

        def ld_C2b(base, sl):
            wo = V(base, 16 * KB, BF16, "p (k n) -> p k n", k=8)
            wd(wo, w_out_v, sl)
            return wo

        def ld_D(fg):
            def fn(base, sl):
                w1 = V(base, 8 * KB, BF16, "p (k n) -> p k n", k=8)
                w2 = V(base + 8 * KB, 8 * KB, BF16, "p (k n) -> p k n", k=4)
                wd(w1, w_m1_v[:, :, fg * 512:(fg + 1) * 512], sl)
                wd(w2, w_m2_v[:, 4 * fg:4 * fg + 4, :], sl)
                return (w1, w2)
            return fn

        wl_plan(("B", 2), 1, ld_B(2), slot=lambda: 1)
        for gi_ in (1, 0):
            wl_plan(("B", gi_), 1, ld_B(gi_))
        wl_plan(("C1",), 2, ld_C1)
        for jp_ in range(4):
            wl_plan(("C2a", jp_), 3, ld_C2a(jp_))
        wl_plan(("C2b",), 4, ld_C2b)
        dslots = {}

        def d_slot(fg):
            def f():
                s_wo = wl["res"][("C2b",)][0]
                return [1 - s_wo, s_wo, 2][fg % 3]
            return f
        for fg_ in range(8):
            wl_plan(("D", fg_), 5, ld_D(fg_), early=(fg_ == 0), slot=d_slot(fg_))

        pools = {"all": [0, 1, 2, 3, 4, 5, 6, 7], "s": [0, 1, 2, 7], "pv": [3, 4, 5, 6]}
        pool_ctr = {k: 0 for k in pools}

        def bank(pool="all"):
            i = pools[pool][pool_ctr[pool] % len(pools[pool])]
            pool_ctr[pool] += 1
            return i

        rr = [0]

        def evac_eng():
            rr[0] += 1
            return "act" if rr[0] % 2 else "dve"

        def copy_op(eng, out, in_, reads, writes, scale=None):
            if eng == "act":
                if scale is None:
                    P.add("act", I("activation", out=out, in_=in_, func=AF.Copy), reads, writes)
                else:
                    P.add("act", I("activation", out=out, in_=in_, func=AF.Copy, scale=scale), reads, writes)
            else:
                if scale is None:
                    P.add("dve", I("tensor_copy", out=out, in_=in_), reads, writes)
                else:
                    P.add("dve", I("tensor_scalar", out=out, in0=in_, scalar1=scale, scalar2=None,
                                                           op0=ALU.mult), reads, writes)

        def mm(out, lhsT, rhs, start, stop, reads, writes):
            P.add("pe", I("matmul", out, lhsT=lhsT, rhs=rhs, start=start, stop=stop), reads, writes)

        def dense(out_fn, pairs, reads, pool="all"):
            b = bank(pool)
            o = out_fn(psf[b])
            n = len(pairs)
            for i, (l, r) in enumerate(pairs):
                mm(o, l, r, i == 0, i == n - 1, reads, [("ps", b)])
            return b

        def wdma(out, in_, keys, dkey):
            P.add("pool", I("dma_start", out=out, in_=in_), writes=keys, dma_key=dkey)

        def dump(name, ap, shape, dt, reads):
            if not debug:
                return
            t = dram("dbg_" + name, shape, dt, out=True)
            dbg[name] = t
            P.add("sp", I("dma_start", out=t, in_=ap), reads=reads, dma_key="dbg_" + name, is_out=True)

        try:
            P.add("sp", I("dma_start", out=ident_f, in_=ident_d), writes=["ident_f"], dma_key="c0")
            P.add("pool", I("dma_start", out=ident_b, in_=ident_d), writes=["ident_b"], dma_key="c1")
            P.add("sp", I("dma_start", out=hbias, in_=hbias_d), writes=["hbias"], dma_key="c2")
            P.add("dve", I("memset", eps_c, 1e-6), writes=["eps"])

            o = PB
            xs = [V(o + i * 4 * KB, 4 * KB) for i in range(12)]; o += 48 * KB
            t_order = [4, 5, 0, 1, 2, 3, 6, 7]
            xs_base = {t_: (i_ % 3) * 4 for i_, t_ in enumerate(t_order)}
            xn = [V(o + i * 2 * KB, 2 * KB, BF16) for i in range(2)]; o += 8 * KB
            junk = V(o, 2 * KB, BF16); o += 2 * KB
            uTh = [V(o + i * 8 * KB, 8 * KB, BF16, "p (k t) -> p k t", k=8) for i in range(2)]; o += 16 * KB
            gmix_b = V(o, 4 * KB); o += 4 * KB
            TOP = 162 * KB
            KT = V(TOP, 16 * KB, BF16, "p (c t) -> p c t", c=2)
            VT = V(TOP + 16 * KB, 16 * KB, BF16, "p (c t) -> p c t", c=2)
            sK2 = V(TOP + 32 * KB, 2 * KB, BF16, "p (c t) -> p c t", c=2)
            sV2 = V(TOP + 34 * KB, 2 * KB, BF16, "p (c t) -> p c t", c=2)
            sK1 = V(TOP + 36 * KB, 512, BF16, "p (c t) -> p c t", c=2)
            sV1 = V(TOP + 36 * KB + 512, 512, BF16, "p (c t) -> p c t", c=2)
            aT = V(199 * KB, 8 * KB, BF16, "p (c t) -> p c t", c=2)

            P.add("sp", I("dma_start", out=gmix_b, in_=gmix_d), writes=["gmix_b"], dma_key="c3")
            wsA2 = V(WS_OFF, 24 * KB, BF16, "p (s k n) -> p s k n", s=2, k=8)
            for s_i, col in enumerate([768, 1536]):
                wdma(wsA2[:, s_i], w_in_v[:, :, col:col + 768],
                     [("ws", 0), ("ws", 1), ("ws", 0, "kv"), ("ws", 1, "kv")], "ws")
            wsB3 = V(G3_OFF, 12 * KB, BF16, "p (s k n) -> p s k n", s=3, k=8)

            def wsA_cols(s_i, kc, c):
                g_off = {0: 512, 1: 512, 2: 256, 3: 256, 4: 0, 5: 0}[s_i]
                return wsA2[:, s_i % 2, kc, g_off + c * 128:g_off + (c + 1) * 128]

            def norm_tile(t, src_rows_fn, xs_l, xn_l, ss_t, rs_t, rstd_t, g_b, dst_fn, load=True, tag="x"):
                for s_ in range(4):
                    gs = 4 * t + s_
                    sl = xs_base[gs // 4] + gs % 4
                    if load:
                        P.add("sp", I("dma_start", out=xs_l[sl], in_=src_rows_fn(gs)),
                              writes=[(tag + "s", sl)], dma_key=(tag + "s", sl))
                    P.add("act", I("activation", out=junk, in_=xs_l[sl], func=AF.Square,
                                                                      accum_out=ss_t[:, gs:gs + 1]),
                          reads=[(tag + "s", sl)], writes=["junk", (tag + "ss", gs)])
                P.add("act", I("activation", out=rs_t[:, 4 * t:4 * t + 4], in_=ss_t[:, 4 * t:4 * t + 4],
                                                    func=AF.Sqrt, scale=1.0 / 1024.0, bias=eps_c),
                      reads=[(tag + "ss", 4 * t + i) for i in range(4)] + ["eps"], writes=[(tag + "rs", t)])

            def xn_and_transpose(t, xs_l, xn_l, rstd_t, g_b, gkey, dst_fn, dst_keys_fn, tag="x", rs_t=None, filler=None):
                rs_t = rs if rs_t is None else rs_t
                P.add("dve", I("reciprocal", out=rstd_t[:, 4 * t:4 * t + 4], in_=rs_t[:, 4 * t:4 * t + 4]),
                      reads=[(tag + "rs", t)], writes=[(tag + "rstd", t)])

                def stt(s_):
                    gs = 4 * t + s_
                    sl = xs_base[gs // 4] + gs % 4
                    s2 = gs % 2
                    P.add("dve", I("scalar_tensor_tensor",
                        out=xn_l[s2], in0=xs_l[sl], scalar=rstd_t[:, gs:gs + 1], in1=g_b, op0=ALU.mult, op1=ALU.mult),
                        reads=[(tag + "s", sl), (tag + "rstd", t), gkey], writes=[(tag + "n", s2)])

                stt(0)
                for s_ in range(4):
                    gs = 4 * t + s_
                    s2 = gs % 2
                    if filler is not None:
                        filler(s_)
                    b = bank("all")
                    pbv = psf[b][:, :].bitcast(BF16)
                    for kc in range(8):
                        P.add("pe", I("transpose",
                            pbv[:, kc * 128:(kc + 1) * 128], xn_l[s2][:, kc * 128:(kc + 1) * 128], ident_b),
                            reads=[(tag + "n", s2), "ident_b"], writes=[("ps", b)])
                    if s_ + 1 < 4:
                        stt(s_ + 1)
                    copy_op("act" if s_ % 2 == 0 else "dve", dst_fn(gs),
                            pbv[:, 0:1024].rearrange("p (c t) -> p c t", c=8), [("ps", b)], dst_keys_fn(gs))

            def phaseA_norm(t):
                norm_tile(t, lambda gs: xh[gs * 128:(gs + 1) * 128, :], xs, xn, ss, rs, rstd, gmix_b, None)

            kv_pending = []

            def kv_filler(s_):
                n_ = (len(kv_pending) + (3 - s_)) // (4 - s_)
                for _ in range(n_):
                    kv_pending.pop(0)()

            def phaseA_rest(t):
                if t < 4:
                    hs = t % 2
                    xn_and_transpose(t, xs, xn, rstd, gmix_b, "gmix_b",
                                     lambda gs, hs=hs: uTh[hs][:, 0:8, (gs % 4) * 128:(gs % 4 + 1) * 128],
                                     lambda gs, hs=hs: [("uTh", hs)], filler=kv_filler)
                    jobs = [(0, KT, 512 * t, 512, "KT"), (1, VT, 512 * t, 512, "VT")]
                    if t == 3:
                        jobs += [(2, sK2, 0, 512, "sK2"), (3, sV2, 0, 512, "sV2"),
                                 (4, sK1, 0, 128, "sK1"), (5, sV1, 0, 128, "sV1")]
                    for (s_i, dst, off, n, key) in jobs:
                        for c in range(2):
                            def grp(s_i=s_i, dst=dst, off=off, n=n, key=key, c=c, hs=hs):
                                rhs0 = 512 - n
                                b = dense(lambda ps: ps[:, 0:n],
                                          [(wsA_cols(s_i, kc, c), uTh[hs][:, kc, rhs0:512]) for kc in range(8)],
                                          [("uTh", hs), ("ws", 0), ("ws", 1)])
                                copy_op(evac_eng(), dst[:, c, off:off + n], psf[b][:, 0:n], [("ps", b)], [(key, off // 512)])
                            kv_pending.append(grp)
                    if t == 3:
                        copy_op("act", uThl, uTh[hs][:, :, 384:512], [("uTh", hs)], ["uThl"])
                else:
                    T = t - 4
                    xn_and_transpose(t, xs, xn, rstd, gmix_b, "gmix_b",
                                     lambda gs: uT[:, 0:8, (gs - 16) * 128:(gs - 15) * 128],
                                     lambda gs, T=T: [("uT", T)], filler=kv_filler)

            phaseA_norm(t_order[0])
            for i_, t in enumerate(t_order):
                if i_ + 1 < 8:
                    phaseA_norm(t_order[i_ + 1])
                phaseA_rest(t)
                if t == 0:
                    P.add("pool", I("dma_start", out=wsB3[:, 1], in_=w_in_v[:, :, 768 + 512:768 + 768]),
                          reads=[("uTh", 0)], writes=[("wsB3", "k")], dma_key="wsB3k")
                    P.add("pool", I("dma_start", out=wsB3[:, 2], in_=w_in_v[:, :, 1536 + 512:1536 + 768]),
                          reads=[("uTh", 0)], writes=[("wsB3", "v")], dma_key="wsB3v")
                if t == 6:
                    assert not kv_pending
                    wl_prefetch()
            while kv_pending:
                kv_pending.pop(0)()
            dump("uT", uT, [128, 8, 2048], BF16, [("uT", i) for i in range(4)])
            wl_prefetch()
            P.barrier()
            if upto == 1:
                raise _Stop()
            wl["phase"] = 1

            pools["all"] = [5, 6, 7]
            o = PB
            btab = V(o, 4 * KB, F32, "p (h q) -> p h q", h=4); o += 4 * KB
            Vp = V(o, 32 * KB, BF16, "p (s h n) -> p s h n", s=32, h=4); o += 32 * KB
            QT2 = V(o, 16 * KB, BF16, "p (c h t) -> p c h t", c=2, h=2); o += 16 * KB
            acc = V(o, 32 * KB, F32, "p (h t) -> p h t", h=4); o += 32 * KB
            NST = 2
            sT = [V(o + i * 2 * KB, 2 * KB) for i in range(NST)]; o += NST * 2 * KB
            NPT = 6
            pTl = [V(o + i * KB, KB, BF16) for i in range(NPT)]; o += NPT * KB
            lnd = V(PB + 36 * KB, 8 * KB)
            for q_ in range(2):
                P.add("pool", I("memset", V(PB + 36 * KB + q_ * 8 * KB, 8 * KB), 0.0),
                      writes=["QT2z"] + [("QT", i) for i in range(4)])
            assert o <= TOP
            Vp_flat = V(PB + 4 * KB, 32 * KB, BF16, "p (s n) -> p s n", n=128)
            import os
            KSKIP = os.environ.get("KSKIP", "").split(",")
            if upto == 20:
                raise _Stop()
            if "memset" not in KSKIP:
                ONE2 = float(np.frombuffer(np.uint32(0x3F803F80).tobytes(), dtype=np.float32)[0])
                for q_ in range(4):
                    P.add("pool", I("memset", V(PB + 4 * KB + q_ * 8 * KB, 8 * KB), ONE2),
                          writes=["Vp_ones"] + [("Vp", i) for i in range(32)])
            KVALL = [("KT", i) for i in range(8)]
            VTALL = [("VT", i) for i in range(8)]
            ws_slot_ctr = [0]

            def ws_slot():
                s = ws_slot_ctr[0] % 2
                ws_slot_ctr[0] += 1
                return s

            sctr = [0]
            pctr = [0]
            first_group = True
            for gi in (2, 1, 0):
                d = (1, 4, 16)[gi]
                nb = 16 // d
                hl = 128 * d
                sl, wsB = wl_get(("B", gi))
                P.add("sp", I("dma_start", out=btab, in_=btab_d[:, gi * 1024:(gi + 1) * 1024].rearrange(
                    "p (h q) -> p h q", h=4)), writes=["btab"], dma_key="btab")
                if gi == 1:
                    copy_op("act", KT[:, :, 2048 - 512:2048], sK2, [("sK2", 0)] + KVALL, [("KT", 3)])
                    copy_op("dve", VT[:, :, 2048 - 512:2048], sV2, [("sV2", 0)] + VTALL, [("VT", 3)])
                if gi == 0:
                    copy_op("act", KT[:, :, 2048 - 128:2048], sK1, [("sK1", 0)] + KVALL, [("KT", 3)])
                    copy_op("dve", VT[:, :, 2048 - 128:2048], sV1, [("sV1", 0)] + VTALL, [("VT", 3)])
                def qkv_group(s_i, c, T):
                    if gi == 2 and s_i > 0:
                        wkeys = [("wsB3", "k" if s_i == 1 else "v"), ("ws", sl, "kv")]
                    else:
                        wkeys = [("ws", sl)]
                    b = dense(lambda ps: ps[:, 0:512],
                              [(wsB[:, s_i, kc, c * 128:(c + 1) * 128], uT[:, kc, T * 512:(T + 1) * 512]) for kc in range(8)],
                              [("uT", T)] + wkeys)
                    if s_i == 0:
                        ev_q = evac_eng()
                        for hh in range(2):
                            copy_op(ev_q, QT2[hh * 64:(hh + 1) * 64, c, hh, T * 512:(T + 1) * 512],
                                    psf[b][hh * 64:(hh + 1) * 64, 0:512], [("ps", b), "QT2z"], [("QT", T)], scale=0.125)
                    elif s_i == 1:
                        copy_op(evac_eng(), KT[:, c, 2048 + T * 512:2048 + (T + 1) * 512], psf[b][:, 0:512],
                                [("ps", b)], [("KT", 4 + T)])
                    else:
                        copy_op(evac_eng(), VT[:, c, 2048 + T * 512:2048 + (T + 1) * 512], psf[b][:, 0:512],
                                [("ps", b)], [("VT", 4 + T)])

                def vp_bank(r, c, js):
                    t0 = 2048 - hl + r
                    b = bank("all")
                    ev_e = evac_eng()
                    for qi, j in enumerate(js):
                        tj = t0 + j * 128 * d
                        mm(psf[b][:, qi * 128:(qi + 1) * 128], VT[:, c, tj:tj + 127 * d + 1:d], ident_b,
                           True, True, VTALL + ["ident_b"], [("ps", b)])
                    st0 = r * (nb + 1) + js[0]
                    ns_ = len(js)
                    copy_op(ev_e, Vp[:, st0:st0 + ns_, 2 * c:2 * c + 2, 0:64],
                            psf[b][:, 0:ns_ * 128].rearrange("p (s h n) -> p s h n", s=ns_, h=2),
                            [("ps", b)], [("Vp", st0 + i_) for i_ in range(ns_)])

                pools["all"] = [5, 6, 7, 0, 1, 2]
                for c in range(2):
                    for T in range(4):
                        qkv_group(2, c, T)
                vp_jobs = []
                for r in range(d):
                    for c in range(2):
                        for j0 in range(0, nb + 1, 4):
                            vp_jobs.append((r, c, list(range(j0, min(j0 + 4, nb + 1)))))
                qk_jobs = [(s_i, c, T) for s_i in (0, 1) for c in range(2) for T in range(4)]
                for i_, (s_i, c, T) in enumerate(qk_jobs):
                    qkv_group(s_i, c, T)
                    n_ = (len(vp_jobs) + (len(qk_jobs) - 1 - i_)) // (len(qk_jobs) - i_)
                    for _ in range(n_):
                        vp_bank(*vp_jobs.pop(0))
                assert not vp_jobs
                pools["all"] = [5, 6, 7]
                if upto == 22 + 10 * (2 - gi):
                    raise _Stop()
                QALL = [("QT", i) for i in range(4)] + ["QT2z"]
                ACCALL = [("acc", i) for i in range(4)]
                events = []
                for r in range(d):
                    for c in range(2):
                        for j in range(nb + 1):
                            qlo, qhi = max(j, 1), min(j + 1, nb)
                            events.append(("S", r, c, j, qlo, qhi))
                            if j >= 1:
                                events.append(("PV", r, 2 * c, j))
                                events.append(("PV", r, 2 * c + 1, j))
                CH = 2
                s_evs = [ev for ev in events if ev[0] == "S"]
                s_pos = {(ev[1], ev[2], ev[3]): i for i, ev in enumerate(s_evs)}
                pv_by_chunk = {}
                for ev in events:
                    if ev[0] == "PV":
                        pv_by_chunk.setdefault(s_pos[(ev[1], ev[2] // 2, ev[3])] // CH, []).append(ev)
                outl = []
                nchunks = (len(s_evs) + CH - 1) // CH
                for k_ in range(nchunks + 1):
                    if k_ < nchunks:
                        outl += s_evs[k_ * CH:(k_ + 1) * CH]
                    if k_ >= 1:
                        outl += pv_by_chunk.get(k_ - 1, [])
                ptile = {}
                pv_states = {}

                def flush_pv(key):
                    st_ = pv_states.get(key)
                    if st_ is None or st_["bank"] is None:
                        return
                    b = st_["bank"]
                    jobs = st_["jobs"]
                    if gi == 2:
                        r = jobs[0][0]
                        dst = acc[:, 0:4, r:r + 127 * 16 + 1:16]
                        src = psf[b][:, :].rearrange("p (h t) -> p h t", h=4)
                        P.add("act", I("activation", out=dst, in_=src, func=AF.Copy),
                              reads=[("ps", b)], writes=ACCALL)
                    else:
                        (r, h, m0) = jobs[0]
                        if gi == 1:
                            dst = acc[:, h, r:r + 511 * 4 + 1:4]
                        else:
                            dst = acc[:, h, (m0 - 1) * 128:(m0 - 1) * 128 + 512]
                        src = psf[b][:, 0:512]
                        P.add("dve", I("tensor_tensor", out=dst, in0=src, in1=dst, op=ALU.add),
                              reads=[("ps", b)] + ACCALL, writes=ACCALL)
                    st_["bank"] = None
                    st_["jobs"] = []

                for ev in outl:
                    if ev[0] == "S":
                        _, r, c, j, qlo, qhi = ev
                        n = 128 * (qhi - qlo + 1)
                        col0 = 0 if qlo == j else 128
                        kt0 = 2048 - hl + j * hl + r
                        q0 = (qlo - 1) * hl + r
                        b = bank("s")
                        si = sctr[0] % NST
                        sctr[0] += 1
                        pi = pctr[0] % NPT
                        pctr[0] += 1
                        ptile[(r, c, j)] = (pi, qlo, n)
                        mm(psf[b][:, 0:2 * n], KT[:, c, kt0:kt0 + 127 * d + 1:d],
                           QT2[:, c, :, q0:q0 + (n - 1) * d + 1:d], True, True, KVALL + QALL, [("ps", b)])
                        P.add("dve", I("tensor_tensor",
                            out=sT[si][:, 0:2 * n].rearrange("p (h q) -> p h q", h=2),
                            in0=psf[b][:, 0:2 * n].rearrange("p (h q) -> p h q", h=2),
                            in1=btab[:, 2 * c:2 * c + 2, col0:col0 + n], op=ALU.add),
                            reads=[("ps", b), "btab"], writes=[("sT", si)])
                        if j == 0:
                            P.add("act", I("activation",
                                out=pTl[pi][:, 0:2 * n], in_=sT[si][:, 0:2 * n], func=AF.Exp, bias=hbias),
                                reads=[("sT", si), "hbias"], writes=[("pT", pi)])
                        else:
                            P.add("act", I("activation",
                                out=pTl[pi][:, 0:2 * n], in_=sT[si][:, 0:2 * n], func=AF.Exp),
                                reads=[("sT", si)], writes=[("pT", pi)])
                    else:
                        _, r, h, m = ev
                        c, hh = h // 2, h % 2
                        key = 0 if gi == 2 else hh
                        st_ = pv_states.setdefault(key, {"bank": None, "jobs": []})
                        if st_["bank"] is None:
                            st_["bank"] = bank("pv")
                        b = st_["bank"]
                        q4 = len(st_["jobs"])
                        st_["jobs"].append((r, h, m))
                        st_prev = r * (nb + 1) + (m - 1)
                        st_cur = r * (nb + 1) + m
                        pi0, qlo0, n0 = ptile[(r, c, m - 1)]
                        pi1, qlo1, n1 = ptile[(r, c, m)]
                        o_ap = psf[b][:, q4 * 128:(q4 + 1) * 128]
                        a0 = hh * n0 + (m - qlo0) * 128
                        a1 = hh * n1 + (m - qlo1) * 128
                        mm(o_ap, Vp[:, st_prev, h, :], pTl[pi0][:, a0:a0 + 128], True, False,
                           [("Vp", st_prev), "Vp_ones", ("pT", pi0)], [("ps", b)])
                        mm(o_ap, Vp[:, st_cur, h, :], pTl[pi1][:, a1:a1 + 128], False, True,
                           [("Vp", st_cur), "Vp_ones", ("pT", pi1)], [("ps", b)])
                        if len(st_["jobs"]) == 4:
                            flush_pv(key)
                for key in list(pv_states):
                    flush_pv(key)
                if upto == 23 + (2 - gi):
                    raise _Stop()
            wl_prefetch()
            P.barrier()
            if upto == 2:
                raise _Stop()
            wl["phase"] = 2
            pools["all"] = [0, 1, 2, 3, 4, 5, 6, 7]
            ACCALL = [("acc", i) for i in range(4)]
            dump("acc", acc, [128, 4, 2048], F32, ACCALL)

            pT = V(PB, 24 * KB, BF16, "p (k t) -> p k t", k=6)
            class _Chunks:
                def __init__(self, views):
                    self.v = views

                def __getitem__(self, idx):
                    p_, ch_, t_ = idx
                    return self.v[ch_][p_, t_]

            pooled = _Chunks([V(a_ * KB, 4 * KB, BF16) for a_ in (92, 96, 100, 112, 116, 186, 190, 194)])
            o = 180 * KB
            ptab = V(o, 2 * KB, BF16, "p (g r t) -> p g r t", g=4, r=2); o += 2 * KB
            ptab0 = V(o, 1 * KB, BF16, "p (g t) -> p g t", g=4); o += 1 * KB
            pscale = V(o, 32); o += 32
            wpad = V(o, 2 * KB, BF16, "p (s a k n) -> p s a k n", s=2, a=2, k=2); o += 2 * KB
            wpad_f = V(o - 2 * KB, 2 * KB)
            pzk = V(152 * KB, 17 * 1536, BF16, "p (b n) -> p b n", n=768)
            sl, (wpz, wgr) = wl_get(("C1",))
            P.add("sp", I("dma_start", out=pscale[:, 0:6], in_=pscale_d), writes=["pscale"], dma_key="c4")
            P.add("pool", I("dma_start", out=ptab, in_=ptab_d.rearrange("p (g r t) -> p g r t", g=4, r=2)),
                  writes=["ptab"], dma_key="c5")
            P.add("pool", I("dma_start", out=ptab0, in_=ptab0_d.rearrange("p (g t) -> p g t", g=4)),
                  writes=["ptab0"], dma_key="c8")
            P.add("dve", I("memset", wpad_f, 0.0), writes=["wpad"])
            for s_ in range(2):
                P.add("pool", I("dma_start", out=wpad[0:96, s_, 0, :, 0:64], in_=w_grp_v[:, 2 * s_, :, 128:192]),
                      writes=["wpad"], dma_key="c9")
                P.add("pool", I("dma_start", out=wpad[0:96, s_, 1, :, 64:128], in_=w_grp_v[:, 2 * s_ + 1, :, 0:64]),
                      writes=["wpad"], dma_key="c9")
            for bl in range(17):
                for hf in range(2):
                    lhs = (lambda kc: uThl[:, kc, :]) if bl == 0 else (lambda kc, bl=bl: uT[:, kc, (bl - 1) * 128:bl * 128])
                    rd = ["uThl"] if bl == 0 else [("uT", (bl - 1) // 4)]
                    b = dense(lambda ps: ps[:, 0:384],
                              [(lhs(kc), wpz[:, kc, hf * 384:(hf + 1) * 384]) for kc in range(8)], rd + [("ws", sl)])
                    copy_op(evac_eng(), pzk[:, bl, hf * 384:(hf + 1) * 384], psf[b][:, 0:384], [("ps", b)], [("pzk", bl)])
                if bl in (3, 6, 9, 12):
                    h = bl // 3 - 1
                    c, pb = h // 2, (h % 2) * 64
                    P.add("act", I("activation", out=lnd[64:128, :], in_=acc[64:128, h, :], func=AF.Ln),
                          reads=ACCALL, writes=["lnd_hi"])
                    P.add("act", I("activation", out=lnd[0:64, :], in_=lnd[64:128, :], func=AF.Exp, scale=-1.0),
                          reads=["lnd_hi"], writes=["lnd_lo"])
                    P.add("dve", I("tensor_tensor",
                        out=aT[pb:pb + 64, c, :], in0=acc[0:64, h, :], in1=lnd[0:64, :], op=ALU.mult),
                        reads=ACCALL + ["lnd_lo"], writes=["aT"])
            dump("aT", aT, [128, 2, 2048], BF16, ["aT"])
            if upto == 3:
                raise _Stop()
            for g in range(4):
                for dc2 in range(2):
                    ch = 2 * g + dc2
                    for q0 in range(4):
                        b = bank("all")
                        for q in range(4):
                            bl = 1 + q0 * 4 + q
                            o_ap = psf[b][0:96, q * 128:(q + 1) * 128]
                            cur = ptab0[:, g, :] if bl == 1 else ptab[:, g, 1, :]
                            mm(o_ap, pzk[:, bl - 1, ch * 96:(ch + 1) * 96], ptab[:, g, 0, :], True, False,
                               [("pzk", bl - 1), "ptab"], [("ps", b)])
                            mm(o_ap, pzk[:, bl, ch * 96:(ch + 1) * 96], cur, False, True,
                               [("pzk", bl), "ptab", "ptab0"], [("ps", b)])
                        copy_op(evac_eng(), pooled[0:96, ch, q0 * 512:(q0 + 1) * 512], psf[b][0:96, 0:512],
                                [("ps", b)], [("pooled", ch, q0)])
                def pl_keys(gg, T):
                    return [("pooled", 2 * gg, T), ("pooled", 2 * gg + 1, T)]

                chunks = {0: [0], 1: [1, 2], 2: [3], 3: [4, 5]}[g]
                for ch6 in chunks:
                    for T in range(4):
                        ts = slice(T * 512, (T + 1) * 512)
                        if ch6 in (1, 4):
                            s_ = ch6 // 3
                            pairs = [(wpad[0:96, s_, 0, kc, :], pooled[0:96, 2 * (2 * s_) + kc, ts]) for kc in range(2)] + \
                                    [(wpad[0:96, s_, 1, kc, :], pooled[0:96, 2 * (2 * s_ + 1) + kc, ts]) for kc in range(2)]
                            rd = pl_keys(2 * s_, T) + pl_keys(2 * s_ + 1, T) + ["wpad"]
                        else:
                            gg = {0: 0, 2: 1, 3: 2, 5: 3}[ch6]
                            c_lo = 0 if ch6 in (0, 3) else 64
                            pairs = [(wgr[0:96, gg, kc, c_lo:c_lo + 128], pooled[0:96, 2 * gg + kc, ts]) for kc in range(2)]
                            rd = pl_keys(gg, T) + [("ws", sl)]
                        b = dense(lambda ps: ps[:, 0:512], pairs, rd)
                        P.add("act", I("activation",
                            out=pT[:, ch6, ts], in_=psf[b][:, 0:512], func=AF.Copy, scale=pscale[:, ch6:ch6 + 1]),
                            reads=[("ps", b), "pscale"], writes=["pT"])
            dump("pT", pT, [128, 6, 2048], BF16, ["pT"])
            wl_prefetch()
            P.barrier()
            if upto == 4:
                raise _Stop()
            wl["phase"] = 3

            o = PB + 32 * KB
            mergedT = V(o, 32 * KB, BF16, "p (k t) -> p k t", k=8); o += 32 * KB
            sg = [[V(o + (i * 4 + j) * 2 * KB, 2 * KB) for j in range(4)] for i in range(2)]; o += 16 * KB
            it = 0
            for jp in range(4):
                sl, (wga, wgp, wat, wpo) = wl_get(("C2a", jp))
                for je in range(2):
                    j = 2 * jp + je
                    cs = slice(je * 128, (je + 1) * 128)
                    for T in range(4):
                        ts = slice(T * 512, (T + 1) * 512)
                        bga = dense(lambda ps: ps[:, 0:512], [(wga[:, kc, cs], uT[:, kc, ts]) for kc in range(8)],
                                    [("uT", T), ("ws", sl)])
                        bgp = dense(lambda ps: ps[:, 0:512], [(wgp[:, kc, cs], uT[:, kc, ts]) for kc in range(8)],
                                    [("uT", T), ("ws", sl)])
                        ba = dense(lambda ps: ps[:, 0:512], [(wat[:, kc, cs], aT[:, kc, ts]) for kc in range(2)],
                                   ["aT", ("ws", sl)])
                        bp = dense(lambda ps: ps[:, 0:512], [(wpo[:, kc, cs], pT[:, kc, ts]) for kc in range(6)],
                                   ["pT", ("ws", sl)])
                        s_ = sg[it % 2]
                        ik = it % 2
                        it += 1
                        P.add("act", I("activation", out=s_[0], in_=psf[bga][:, 0:512], func=AF.Sigmoid),
                              reads=[("ps", bga)], writes=[("sg", ik, 0)])
                        P.add("act", I("activation", out=s_[1], in_=psf[bgp][:, 0:512], func=AF.Sigmoid),
                              reads=[("ps", bgp)], writes=[("sg", ik, 1)])
                        P.add("dve", I("tensor_tensor", out=s_[2], in0=psf[ba][:, 0:512], in1=s_[0], op=ALU.mult),
                              reads=[("ps", ba), ("sg", ik, 0)], writes=[("sg", ik, 2)])
                        P.add("dve", I("tensor_tensor", out=s_[3], in0=psf[bp][:, 0:512], in1=s_[1], op=ALU.mult),
                              reads=[("ps", bp), ("sg", ik, 1)], writes=[("sg", ik, 3)])
                        P.add("dve", I("tensor_tensor", out=mergedT[:, j, ts], in0=s_[2], in1=s_[3], op=ALU.add),
                              reads=[("sg", ik, 2), ("sg", ik, 3)], writes=[("mergedT", T)])
            dump("mergedT", mergedT, [128, 8, 2048], BF16, [("mergedT", i) for i in range(4)])
            wl_prefetch()
            P.barrier()
            if upto == 5:
                raise _Stop()
            wl["phase"] = 4

            hbuf = V(132 * KB, 64 * KB, F32, "p (s n) -> p s n", s=16)
            o = PB
            mn = [V(o + i * 2 * KB, 2 * KB, BF16) for i in range(2)]; o += 8 * KB
            gmlp_b = V(o, 4 * KB); o += 4 * KB
            junk = V(o, 2 * KB, BF16); o += 2 * KB
            P.add("sp", I("dma_start", out=gmlp_b, in_=gmlp_d), writes=["gmlp_b"], dma_key="c6")
            sl, wo = wl_get(("C2b",))
            for s_ in range(16):
                P.add("sp", I("dma_start", out=hbuf[:, s_, :], in_=xh[2048 + s_ * 128:2048 + (s_ + 1) * 128, :]),
                      writes=[("h", s_)], dma_key=("hld", s_))
            mT = uT
            def c2b_dense_sub(s_):
                T = s_ // 4
                for half in range(2):
                    es_ = slice(half * 512, (half + 1) * 512)
                    b = dense(lambda ps: ps[:, 0:512],
                              [(mergedT[:, kc, s_ * 128:(s_ + 1) * 128], wo[:, kc, es_]) for kc in range(8)],
                              [("mergedT", T), ("ws", sl)])
                    P.add("dve", I("tensor_tensor",
                        out=hbuf[:, s_, es_], in0=psf[b][:, 0:512], in1=hbuf[:, s_, es_], op=ALU.add),
                        reads=[("ps", b), ("h", s_)], writes=[("h", s_)])
                P.add("act", I("activation", out=junk, in_=hbuf[:, s_, :], func=AF.Square,
                               accum_out=ss2[:, s_:s_ + 1]),
                      reads=[("h", s_)], writes=["junk", ("ss2", s_)])
                if s_ % 4 == 3:
                    P.add("act", I("activation", out=rs2[:, 4 * T:4 * T + 4], in_=ss2[:, 4 * T:4 * T + 4],
                                   func=AF.Sqrt, scale=1.0 / 1024.0, bias=eps_c),
                          reads=[("ss2", 4 * T + i) for i in range(4)] + ["eps"], writes=[("rs2", T)])

            def c2b_stt(sq):
                T = sq // 4
                s2 = sq % 2
                if sq % 4 == 0:
                    P.add("dve", I("reciprocal", out=rstd2[:, 4 * T:4 * T + 4], in_=rs2[:, 4 * T:4 * T + 4]),
                          reads=[("rs2", T)], writes=[("rstd2", T)])
                P.add("dve", I("scalar_tensor_tensor",
                    out=mn[s2], in0=hbuf[:, sq, :], scalar=rstd2[:, sq:sq + 1], in1=gmlp_b, op0=ALU.mult, op1=ALU.mult),
                    reads=[("h", sq), ("rstd2", T), "gmlp_b"], writes=[("mn", s2)])

            def c2b_tr(sq):
                T = sq // 4
                s2 = sq % 2
                b = bank("all")
                pbv = psf[b][:, :].bitcast(BF16)
                for kc in range(8):
                    P.add("pe", I("transpose",
                        pbv[:, kc * 128:(kc + 1) * 128], mn[s2][:, kc * 128:(kc + 1) * 128], ident_b),
                        reads=[("mn", s2), "ident_b"], writes=[("ps", b)])
                return b, pbv

            def c2b_evac(sq, b, pbv):
                T = sq // 4
                copy_op("act" if sq % 2 == 0 else "dve", mT[:, 0:8, sq * 128:(sq + 1) * 128],
                        pbv[:, 0:1024].rearrange("p (c t) -> p c t", c=8), [("ps", b)], [("uT", T)])

            for s_ in range(4):
                c2b_dense_sub(s_)
            c2b_stt(0)
            for sq in range(16):
                if sq + 4 < 16:
                    c2b_dense_sub(sq + 4)
                b_, pbv_ = c2b_tr(sq)
                if sq + 1 < 16:
                    c2b_stt(sq + 1)
                c2b_evac(sq, b_, pbv_)
            dump("h", hbuf, [128, 16, 1024], F32, [("h", i) for i in range(16)])
            wl_prefetch()
            P.barrier()
            if upto == 6:
                raise _Stop()
            wl["phase"] = 5

            o = PB
            hidT = [V(o + i * 16 * KB, 16 * KB, BF16, "p (k t) -> p k t", k=4) for i in range(2)]; o += 32 * KB
            rtmp = [V(o + i * 2 * KB, 2 * KB) for i in range(2)]; o += 4 * KB
            junk = V(o, 2 * KB, BF16); o += 2 * KB
            assert o <= S3_OFF
            gfin_b = V(122 * KB, 4 * KB)
            P.add("sp", I("dma_start", out=gfin_b, in_=gfin_d), writes=["gfin_b"], dma_key="c7")
            NFG = 8
            wsl = {}
            wl["auto"] = False

            def load_fg(fg):
                sl, (w1, w2) = wl_get(("D", fg))
                wsl[fg] = (sl, w1, w2)

            rc = [0]

            def mlp_in(fg):
                sl, w1, w2 = wsl[fg]
                hT = hidT[fg % 2]
                for fc in range(4):
                    for T in range(4):
                        ts = slice(T * 512, (T + 1) * 512)
                        b = dense(lambda ps: ps[:, 0:512],
                                  [(w1[:, kc, fc * 128:(fc + 1) * 128], mT[:, kc, ts]) for kc in range(8)],
                                  [("uT", T), ("ws", sl)])
                        ri = rc[0] % 2
                        rc[0] += 1
                        P.add("act", I("activation", out=rtmp[ri], in_=psf[b][:, 0:512], func=AF.Relu),
                              reads=[("ps", b)], writes=[("rtmp", ri)])
                        P.add("dve", I("scalar_tensor_tensor",
                            out=hT[:, fc, ts], in0=psf[b][:, 0:512], scalar=0.0, in1=rtmp[ri], op0=ALU.max, op1=ALU.mult),
                            reads=[("ps", b), ("rtmp", ri)], writes=[("hidT", fg % 2, T)])

            def final_norm(T):
                for s_ in range(4 * T, 4 * T + 4):
                    P.add("act", I("activation", out=junk, in_=hbuf[:, s_, :], func=AF.Square,
                                   accum_out=ss3[:, s_:s_ + 1]),
                          reads=[("h", s_)], writes=["junk", ("ss3", s_)])
                P.add("act", I("activation", out=rs3[:, 4 * T:4 * T + 4], in_=ss3[:, 4 * T:4 * T + 4],
                               func=AF.Sqrt, scale=1.0 / 1024.0, bias=eps_c),
                      reads=[("ss3", 4 * T + i) for i in range(4)] + ["eps"], writes=[("rs3", T)])
                P.add("dve", I("reciprocal", out=rstd3[:, 4 * T:4 * T + 4], in_=rs3[:, 4 * T:4 * T + 4]),
                      reads=[("rs3", T)], writes=[("rstd3", T)])
                for sq in range(4 * T, 4 * T + 4):
                    P.add("dve", I("scalar_tensor_tensor",
                        out=hbuf[:, sq, :], in0=hbuf[:, sq, :], scalar=rstd3[:, sq:sq + 1], in1=gfin_b,
                        op0=ALU.mult, op1=ALU.mult),
                        reads=[("h", sq), ("rstd3", T), "gfin_b"], writes=[("h", sq)])
                    P.add("sp", I("dma_start", out=y[sq * 128:(sq + 1) * 128, :], in_=hbuf[:, sq, :]),
                          reads=[("h", sq)], dma_key=("yst", sq), is_out=True)

            def mlp_out(fg):
                sl, w1, w2 = wsl[fg]
                hT = hidT[fg % 2]
                for s_ in range(16):
                    T = s_ // 4
                    for half in range(2):
                        es_ = slice(half * 512, (half + 1) * 512)
                        b = dense(lambda ps: ps[:, 0:512],
                                  [(hT[:, fc, s_ * 128:(s_ + 1) * 128], w2[:, fc, es_]) for fc in range(4)],
                                  [("hidT", fg % 2, T), ("ws", sl)])
                        P.add("dve", I("tensor_tensor",
                            out=hbuf[:, s_, es_], in0=psf[b][:, 0:512], in1=hbuf[:, s_, es_], op=ALU.add),
                            reads=[("ps", b), ("h", s_)], writes=[("h", s_)])
                    if fg == NFG - 1 and s_ % 4 == 3 and T >= 1:
                        final_norm(T - 1)
                if fg == NFG - 1:
                    final_norm(3)

            load_fg(0)
            load_fg(1)
            load_fg(2)
            mlp_in(0)
            for fg in range(NFG):
                if fg + 1 < NFG:
                    mlp_in(fg + 1)
                mlp_out(fg)
                if fg + 3 < NFG:
                    load_fg(fg + 3)

        except _Stop:
            pass
        names = ["eng_" + e for e in Prog.ENGS] + ["dma_%d" % i for i in range(len(P.dma_keys))]
        sems = {n: es.enter_context(nc.semaphore(n)) for n in names}
        block = es.enter_context(nc.Block())
        P.emit(nc, block, sems)
    return nc, dbg


def host_inputs(x, norm_mix_g, w_in, w_att_out, w_pool_grp, pool_scale, w_pool_out, w_out,
                norm_mlp_g, w_mlp_in, w_mlp_out, norm_final_g):
    f = np.float32
    x2 = np.asarray(x, f)[0]
    common = {
        "w_in": np.ascontiguousarray(np.asarray(w_in, f)[0]),
        "w_att": np.ascontiguousarray(np.asarray(w_att_out, f)[0]),
        "w_grp": np.ascontiguousarray(np.asarray(w_pool_grp, f)[0].reshape(768, 192)),
        "w_po": np.ascontiguousarray(np.asarray(w_pool_out, f)[0]),
        "w_out": np.ascontiguousarray(np.asarray(w_out, f)[0]),
        "w_m1": np.ascontiguousarray(np.asarray(w_mlp_in, f)[0]),
        "w_m2": np.ascontiguousarray(np.asarray(w_mlp_out, f)[0]),
        "gmix_b": np.ascontiguousarray(np.broadcast_to(np.asarray(norm_mix_g, f)[0], (128, 1024))),
        "gmlp_b": np.ascontiguousarray(np.broadcast_to(np.asarray(norm_mlp_g, f)[0], (128, 1024))),
        "gfin_b": np.ascontiguousarray(np.broadcast_to(np.asarray(norm_final_g, f), (128, 1024))),
        "pscale": np.ascontiguousarray(np.asarray(pool_scale, f)[0].reshape(6, 128).T),
        "ident": np.eye(128, dtype=f),
    }
    slopes = 2.0 ** (-8.0 * (np.arange(12, dtype=np.float64) + 1.0) / 12.0)
    k_i = np.arange(128)[:, None]
    q_i = np.arange(256)[None, :]
    steps = q_i - k_i
    valid = (steps >= 0) & (steps <= 128)
    bt = np.zeros((128, 12, 256), f)
    for hd in range(12):
        dil = (1, 4, 16)[hd // 4]
        bt[:, hd, :] = np.where(valid, -(slopes[hd] * steps * dil), NEG).astype(f)
    common["btab"] = np.ascontiguousarray(bt.reshape(128, 12 * 256))
    tk = np.arange(128)[:, None]
    tq = np.arange(128)[None, :]
    pt = np.zeros((128, 4, 2, 128), f)
    for g, w in enumerate((2, 4, 8, 16)):
        dlt = tq - tk
        pt[:, g, 1, :] = np.where((dlt >= 0) & (dlt <= w - 1), 1.0 / w, 0.0) - (dlt == 0)
        dlp = tq + 128 - tk
        pt[:, g, 0, :] = np.where(dlp <= w - 1, 1.0 / w, 0.0)
    common["ptab"] = np.ascontiguousarray(pt.reshape(128, 1024))
    in_maps = []
    for c in range(NCORES):
        xhc = np.zeros((4096, 1024), f)
        if c > 0:
            xhc[0:2048] = x2[(c - 1) * TOK:c * TOK]
        xhc[2048:] = x2[c * TOK:(c + 1) * TOK]
        hb = np.full((128, 1), NEG if c == 0 else 0.0, f)
        p0 = np.zeros((128, 4, 128), f)
        for g, w in enumerate((2, 4, 8, 16)):
            dlt = tq - tk
            cnt = np.minimum(tq + 1, w) if c == 0 else w
            p0[:, g, :] = np.where((dlt >= 0) & (dlt <= w - 1), 1.0 / cnt, 0.0) - (dlt == 0)
        m = dict(common)
        m["xh"] = xhc
        m["hbias"] = hb
        m["ptab0"] = np.ascontiguousarray(p0.reshape(128, 512))
        in_maps.append(m)
    return in_maps


_CACHE = {}


def kernel(**inputs):
    in_maps = host_inputs(**inputs)
    if "nc" not in _CACHE:
        _CACHE["nc"] = build(False)[0]
    nc = _CACHE["nc"]
    res = run_bass_kernel_spmd(nc, in_maps, core_ids=list(range(NCORES)))
    out = np.concatenate([np.asarray(r["y"], np.float32) for r in res.results], axis=0)
    return out.reshape(1, NCORES * TOK, 1024)
```
